# Optimizing a Trainium2 kernel written in Bass

```python
import math
import jax, jax.numpy as jnp
from jax import lax
import numpy as np

D_MODEL = 1024
BATCH = 32
SEQ = 256
DEPTH = 4
DEC_BATCH = 2
DEC_SEQ = 1024
PAST_LEN = 256

GRID_W = 64
GROUP_W = D_MODEL // 4
A_HEADS = 4
A_V_DIM = GROUP_W // A_HEADS
A_QK_DIM = A_V_DIM // 2
B_GROUPS = 4
CHUNK = 128
C_KERNEL = 31
D_GROUPS = 4
D_FF = 4 * D_MODEL
ROPE_BASE = 10000.0
Q_BLOCK = 128
EPS = 1e-6
IN_W = 8 * GROUP_W
SPLITS = (GROUP_W, 2 * GROUP_W, 3 * GROUP_W, 5 * GROUP_W, 7 * GROUP_W)

kernel_name = "hybrid_diffusion_prefix_trunk_step"

F32 = jnp.float32


def rms_norm(x, g):
    xf = x.astype(F32)
    y = xf * lax.rsqrt(jnp.mean(xf * xf, axis=-1, keepdims=True) + EPS)
    return (y * g.astype(F32)).astype(x.dtype)


def layer_norm(x, g, b):
    xf = x.astype(F32)
    mu = jnp.mean(xf, axis=-1, keepdims=True)
    var = jnp.mean(jnp.square(xf - mu), axis=-1, keepdims=True)
    y = (xf - mu) * lax.rsqrt(var + EPS) * g.astype(F32) + b.astype(F32)
    return y.astype(x.dtype)


def axial_rope_tables(n_tokens):
    rows = n_tokens // GRID_W
    row = jnp.repeat(jnp.arange(rows, dtype=F32), GRID_W)
    col = jnp.tile(jnp.arange(GRID_W, dtype=F32), rows)
    nf = A_QK_DIM // 4
    inv = ROPE_BASE ** (-jnp.arange(nf, dtype=F32) / nf)
    ang = jnp.stack([row[:, None] * inv, col[:, None] * inv], axis=1)
    return jnp.cos(ang), jnp.sin(ang)


def apply_rope(x, cos, sin):
    shp = x.shape
    xr = x.astype(F32).reshape(shp[:-1] + (2, 2, A_QK_DIM // 4))
    a, b = xr[..., 0, :], xr[..., 1, :]
    cs, sn = cos[:, None, None], sin[:, None, None]
    rot = jnp.stack([a * cs - b * sn, b * cs + a * sn], axis=-2)
    return rot.reshape(shp).astype(x.dtype)


def diff_attention(q, k, v, lam):
    bn, lq = q.shape[:2]
    nb = lq // Q_BLOCK
    qb = q.reshape((bn, nb, Q_BLOCK) + q.shape[2:]).swapaxes(0, 1)
    kf, vf = k.astype(F32), v.astype(F32)
    scale = A_QK_DIM ** -0.5

    def block(qi):
        s = jnp.einsum('bqhcd,bkhcd->bchqk', qi.astype(F32), kf) * scale
        p = jax.nn.softmax(s, axis=-1)
        w = p[:, 0] - lam * p[:, 1]
        return jnp.einsum('bhqk,bkhd->bqhd', w, vf)

    o = lax.map(block, qb)
    return o.swapaxes(0, 1).reshape(bn, lq, A_HEADS, A_V_DIM)


def spatial_gating(u, v, g, b, w_s, b_s):
    bn, L, _ = u.shape
    vn = layer_norm(v, g, b)
    vc = vn.reshape(bn, L // CHUNK, CHUNK, B_GROUPS, GROUP_W // B_GROUPS)
    mixed = jnp.einsum('gpq,bnqgc->bnpgc', w_s, vc) + b_s.T[None, None, :, :, None]
    return u * mixed.reshape(bn, L, GROUP_W)


def conv_module(a, gate, w_dw, b_dw, g, b):
    h = a * jax.nn.sigmoid(gate)
    y = lax.conv_general_dilated(
        h, w_dw[:, None, :], window_strides=(1,),
        padding=[(C_KERNEL // 2, C_KERNEL // 2)],
        dimension_numbers=('NWC', 'WIO', 'NWC'),
        feature_group_count=GROUP_W) + b_dw
    return jax.nn.silu(layer_norm(y, g, b))


def fourier_mix(z):
    bn, L, _ = z.shape
    zg = z.reshape(bn, L, D_GROUPS, GROUP_W // D_GROUPS).astype(F32)
    f = jnp.fft.fftn(zg, axes=(1, 3), norm='ortho').real
    return f.reshape(bn, L, GROUP_W).astype(z.dtype)


def trunk_layer(x, cond, P, l, rope, ctx_k, ctx_v):
    bn, L, _ = x.shape
    mod = (jax.nn.silu(cond) @ P['w_ada'][l] + P['b_ada'][l])[:, None, :]
    sh1, sc1, gt1, sh2, sc2, gt2 = jnp.split(mod, 6, axis=-1)
    h = rms_norm(x, P['g_attn_norm'][l]) * (1 + sc1) + sh1
    z = h @ P['w_in'][l]
    zq, zk, zv, zb, zc, zd = jnp.split(z, SPLITS, axis=-1)

    q = rms_norm(zq.reshape(bn, L, A_HEADS, 2, A_QK_DIM), P['g_q'][l])
    k = rms_norm(zk.reshape(bn, L, A_HEADS, 2, A_QK_DIM), P['g_k'][l])
    v = zv.reshape(bn, L, A_HEADS, A_V_DIM)
    own_k, own_v = k, v
    if rope is not None:
        q = apply_rope(q, *rope)
        k = apply_rope(k, *rope)
    if ctx_k is not None:
        k = jnp.concatenate([ctx_k.astype(k.dtype), k], axis=1)
        v = jnp.concatenate([ctx_v.astype(v.dtype), v], axis=1)
    lam_init = 0.8 - 0.6 * math.exp(-0.3 * l)
    lam = (jnp.exp(jnp.sum(P['lam_q1'][l].astype(F32) * P['lam_k1'][l].astype(F32)))
           - jnp.exp(jnp.sum(P['lam_q2'][l].astype(F32) * P['lam_k2'][l].astype(F32)))
           + lam_init)
    o_a = rms_norm(diff_attention(q, k, v, lam), P['g_head'][l]) * (1.0 - lam_init)
    o_a = o_a.reshape(bn, L, GROUP_W).astype(x.dtype)

    ub, vb = jnp.split(jax.nn.gelu(zb), 2, axis=-1)
    o_b = spatial_gating(ub, vb, P['g_sg'][l], P['b_sg'][l], P['w_spatial'][l], P['b_spatial'][l])

    ac, gc = jnp.split(zc, 2, axis=-1)
    o_c = conv_module(ac, gc, P['w_dw'][l], P['b_dw'][l], P['g_conv'][l], P['b_conv'][l])

    o_d = fourier_mix(zd)

    mix = jnp.concatenate([o_a, o_b, o_c, o_d], axis=-1) @ P['w_out'][l]
    x = x + gt1 * mix
    h2 = rms_norm(x, P['g_mlp_norm'][l]) * (1 + sc2) + sh2
    ff = jnp.square(jax.nn.relu(h2 @ P['w_ff1'][l])) @ P['w_ff2'][l]
    x = x + gt2 * ff
    return x, own_k, own_v


def setup_inputs(seed: int = 0) -> dict:
    key = jax.random.key(seed)
    ks = jax.random.split(key, 32)
    D = D_MODEL

    def nrm(k, shape, s):
        return jax.random.normal(k, shape, F32) * s

    return {
        'x_prompt': nrm(ks[0], (BATCH, SEQ, D), 1.0),
        'x_sample': nrm(ks[1], (DEC_BATCH, DEC_SEQ, D), 1.0),
        'c': nrm(ks[2], (DEC_BATCH, D), 1.0),
        'cache_k': nrm(ks[3], (DEC_BATCH, DEPTH, PAST_LEN, A_HEADS, 2 * A_QK_DIM), 1.0),
        'cache_v': nrm(ks[4], (DEC_BATCH, DEPTH, PAST_LEN, A_HEADS, A_V_DIM), 1.0),
        'c_ctx': nrm(ks[5], (D,), 1.0),
        'w_ada': nrm(ks[6], (DEPTH, D, 6 * D), D ** -0.5),
        'b_ada': nrm(ks[7], (DEPTH, 6 * D), 0.02),
        'g_attn_norm': 1.0 + nrm(ks[8], (DEPTH, D), 0.05),
        'g_mlp_norm': 1.0 + nrm(ks[9], (DEPTH, D), 0.05),
        'w_in': nrm(ks[10], (DEPTH, D, IN_W), D ** -0.5),
        'g_q': 1.0 + nrm(ks[11], (DEPTH, A_QK_DIM), 0.05),
        'g_k': 1.0 + nrm(ks[12], (DEPTH, A_QK_DIM), 0.05),
        'lam_q1': nrm(ks[13], (DEPTH, A_QK_DIM), 0.1),
        'lam_k1': nrm(ks[14], (DEPTH, A_QK_DIM), 0.1),
        'lam_q2': nrm(ks[15], (DEPTH, A_QK_DIM), 0.1),
        'lam_k2': nrm(ks[16], (DEPTH, A_QK_DIM), 0.1),
        'g_head': 1.0 + nrm(ks[17], (DEPTH, A_V_DIM), 0.05),
        'g_sg': 1.0 + nrm(ks[18], (DEPTH, GROUP_W), 0.05),
        'b_sg': nrm(ks[19], (DEPTH, GROUP_W), 0.02),
        'w_spatial': nrm(ks[20], (DEPTH, B_GROUPS, CHUNK, CHUNK), CHUNK ** -0.5),
        'b_spatial': 1.0 + nrm(ks[21], (DEPTH, B_GROUPS, CHUNK), 0.05),
        'w_dw': nrm(ks[22], (DEPTH, C_KERNEL, GROUP_W), C_KERNEL ** -0.5),
        'b_dw': nrm(ks[23], (DEPTH, GROUP_W), 0.02),
        'g_conv': 1.0 + nrm(ks[24], (DEPTH, GROUP_W), 0.05),
        'b_conv': nrm(ks[25], (DEPTH, GROUP_W), 0.02),
        'w_out': nrm(ks[26], (DEPTH, D, D), D ** -0.5),
        'w_ff1': nrm(ks[27], (DEPTH, D, D_FF), D ** -0.5),
        'w_ff2': nrm(ks[28], (DEPTH, D_FF, D), D_FF ** -0.5),
    }


def reference(x_prompt, x_sample, c, cache_k, cache_v, c_ctx, w_ada, b_ada,
              g_attn_norm, g_mlp_norm, w_in, g_q, g_k, lam_q1, lam_k1, lam_q2, lam_k2,
              g_head, g_sg, b_sg, w_spatial, b_spatial, w_dw, b_dw, g_conv, b_conv,
              w_out, w_ff1, w_ff2):
    P = dict(w_ada=w_ada, b_ada=b_ada, g_attn_norm=g_attn_norm, g_mlp_norm=g_mlp_norm,
             w_in=w_in, g_q=g_q, g_k=g_k, lam_q1=lam_q1, lam_k1=lam_k1, lam_q2=lam_q2,
             lam_k2=lam_k2, g_head=g_head, g_sg=g_sg, b_sg=b_sg, w_spatial=w_spatial,
             b_spatial=b_spatial, w_dw=w_dw, b_dw=b_dw, g_conv=g_conv, b_conv=b_conv,
             w_out=w_out, w_ff1=w_ff1, w_ff2=w_ff2)
    rope = axial_rope_tables(x_sample.shape[1])
    ctx_cond = c_ctx[None, :]
    n_p, l_p = x_prompt.shape[:2]
    n_s, l_c = cache_k.shape[0], cache_k.shape[2]
    xp, xs = x_prompt, x_sample
    k_list, v_list = [], []
    for l in range(DEPTH):
        xp, k_l, v_l = trunk_layer(xp, ctx_cond, P, l, None, None, None)
        k_list.append(k_l.reshape(n_p, l_p, A_HEADS, 2 * A_QK_DIM))
        v_list.append(v_l)
        ck = cache_k[:, l].reshape(n_s, l_c, A_HEADS, 2, A_QK_DIM)
        xs, _, _ = trunk_layer(xs, c, P, l, rope, ck, cache_v[:, l])
    new_k = jnp.stack(k_list, axis=1)
    new_v = jnp.stack(v_list, axis=1)
    return (xp, xs, new_k, new_v)
```

```python
import math
from contextlib import ExitStack

import numpy as np
import ml_dtypes

import concourse.bass as bass
import concourse.mybir as mybir
from concourse.bass_utils import run_bass_kernel_spmd

F32 = mybir.dt.float32
BF16 = mybir.dt.bfloat16
AF = mybir.ActivationFunctionType
ALU = mybir.AluOpType
AX = mybir.AxisListType

D = 1024
DEPTH = 4
NT = 1280
NTT = 10
BLK = [(0, 512), (512, 512), (1024, 256)]
EPS = 1e-6
NSLOT = 4
BIG = 30000.0
QSCALE = 32 ** -0.5
DUMMY_N = 0

O_BADA, O_GATT, O_GMLP, O_GQK, O_GSG, O_BSG, O_GHEAD, O_LAM = 0, 48, 56, 64, 128, 384, 640, 641
O_WDW, O_BDW, O_GCV, O_BCV, O_BSP = 769, 831, 833, 835, 837
NF = 1096


class Buf:
    __slots__ = ("name", "lw", "rdc", "rdd")

    def __init__(self, name):
        self.name = name
        self.lw = None
        self.rdc = {}
        self.rdd = []


class DGrp:
    __slots__ = ("name", "sem", "count")

    def __init__(self, name):
        self.name = name
        self.sem = None
        self.count = 0


class Op:
    __slots__ = ("eng", "fn", "waits", "kind", "val", "dgrp")


class Sched:
    ENG = ["pe", "act", "dve", "pool", "sp"]

    def __init__(self, nc, stack):
        self.nc = nc
        self.stack = stack
        self.ops = []
        self.sem = {e: stack.enter_context(nc.semaphore("sem_" + e)) for e in self.ENG}
        self.cnt = {e: 0 for e in self.ENG}
        self.waited = {e: {} for e in self.ENG}
        self.semobj = {("c", e): self.sem[e] for e in self.ENG}
        self.dgrps = []

    def dgrp(self, name):
        g = DGrp(name)
        g.sem = self.stack.enter_context(self.nc.semaphore("dsem_" + name))
        self.semobj[("d", name)] = g.sem
        self.dgrps.append(g)
        return g

    def add(self, eng, fn, reads=(), writes=(), dgrp=None):
        op = Op()
        op.eng = eng
        op.fn = fn
        op.dgrp = dgrp
        op.kind = "d" if dgrp is not None else "c"
        need = {}

        def dep(d):
            if d is None:
                return
            if d.kind == "c":
                if d.eng == eng and eng == "pe" and op.kind == "c":
                    return
                key = ("c", d.eng)
                val = d.val
            else:
                key = ("d", d.dgrp.name)
                val = d.dgrp.count
            if need.get(key, 0) < val:
                need[key] = val

        for b in reads:
            dep(b.lw)
        for b in writes:
            dep(b.lw)
            for r in b.rdc.values():
                dep(r)
            for r in b.rdd:
                dep(r)
        op.waits = []
        w = self.waited[eng]
        for key, val in need.items():
            if w.get(key, 0) >= val:
                continue
            w[key] = val
            op.waits.append((self.semobj[key], val))
        if op.kind == "d":
            dgrp.count += 16
            op.val = dgrp.count
        else:
            self.cnt[eng] += 1
            op.val = self.cnt[eng]
        for b in reads:
            if op.kind == "c":
                b.rdc[eng] = op
            else:
                b.rdd.append(op)
        for b in writes:
            b.lw = op
            b.rdc = {}
            b.rdd = []
        self.ops.append(op)
        return op

    def emit(self):
        nc = self.nc
        with nc.Block() as block:
            decos = {"pe": block.tensor, "act": block.scalar, "dve": block.vector,
                     "pool": block.gpsimd, "sp": block.sync}
            for eng in self.ENG:
                def body(e, eng=eng):
                    for op in self.ops:
                        if op.eng != eng:
                            continue
                        for (sem, v) in op.waits:
                            e.wait_ge(sem, v)
                        ins = op.fn(e)
                        if op.kind == "c":
                            ins.then_inc(self.sem[eng], 1)
                        else:
                            ins.then_inc(op.dgrp.sem, 16)
                    if eng == "sp":
                        for g in self.dgrps:
                            if g.count > 0:
                                e.wait_ge(g.sem, g.count)
                decos[eng](body)


class Ring:
    def __init__(self, items):
        self.items = items
        self.i = 0

    def next(self):
        it = self.items[self.i % len(self.items)]
        self.i += 1
        return it


def bcast(ap, axis, n):
    shp = list(ap.shape)
    shp.insert(axis, n)
    return ap.unsqueeze(axis).broadcast_to(shp)


class _Stop(Exception):
    pass


def build(n_layers=DEPTH, debug=False, stop=None):
    nc = bass.Bass("TRN2", target_bir_lowering=False)
    L = n_layers

    def din(name, shape, dt=F32):
        return nc.dram_tensor(name, list(shape), dt, kind="ExternalInput").ap()

    def dout(name, shape, dt=F32):
        return nc.dram_tensor(name, list(shape), dt, kind="ExternalOutput").ap()

    xT_d = din("xT", [D, NT])
    cond_d = din("condL", [128, 8, 2])
    ck_d = din("ck", [DEPTH, 256, 256])
    cv_d = din("cv", [DEPTH, 256, 256])
    cs_d = din("cs", [128, NTT, 32])
    ind_d = din("indt", [128, 12, 2, 8], BF16)
    flag_d = din("flag", [128, 1])
    csc_d = din("csc", [128, 256], BF16)
    dfa_d = din("dfa", [4, 128, 4096], BF16)
    dfb_d = din("dfb", [128, 2, 2, 256], BF16)
    ident_d = din("ident", [128, 128], BF16)
    pf_d = din("pf", [DEPTH, 128, NF])
    pa_d = din("pa", [DEPTH, 128, 64])
    wsT_d = din("wsT", [DEPTH, 128, 512])
    w_ada = din("w_ada", [DEPTH, D, 6 * D])
    w_in = din("w_in", [DEPTH, D, 2048])
    w_out = din("w_out", [DEPTH, D, D])
    w_ff1 = din("w_ff1", [DEPTH, D, 4096])
    w_ff2 = din("w_ff2", [DEPTH, 4096, D])

    yT_d = dout("yT", [D, NT])
    nk_d = dout("nk", [DEPTH, NT, 256])
    nv_d = dout("nv", [DEPTH, NT, 256])
    dbg_d = {}

    with ExitStack() as st:
        S = Sched(nc, st)

        def sb(name, shape, dt):
            return st.enter_context(nc.sbuf_tensor(name, list(shape), dt))

        xT = sb("xT_s", [128, 8, NT], F32)
        hT = sb("hT_s", [128, 8, NT], BF16)
        bigT = sb("bigT_s", [128, 8, NT], BF16)
        ring = [sb("ring%d" % i, [128, 4096], BF16) for i in range(NSLOT)]
        QT = sb("QT_s", [128, 4, NT], BF16)
        KT = sb("KT_s", [128, 4, NT + 256], BF16)
        Vaug = sb("Vaug_s", [128, 12, 4, 128], BF16)
        qkaug = [sb("qkaug%d" % i, [128, 16, 64], BF16) for i in range(2)]
        ctxa = sb("ctxa_s", [128, 8, 64], BF16)
        uG = sb("uG_s", [128, 2, NT], BF16)
        hcpad = sb("hcpad_s", [128, 2, 5, 286], BF16)
        cacc = sb("cacc_s", [128, 2, 512], F32)
        zdT = sb("zdT_s", [128, 2, NT], BF16)
        vnr = [sb("vnr%d" % i, [128, 256], BF16) for i in range(2)]
        Er = [sb("Er%d" % i, [128, 512], BF16) for i in range(4)]
        tF = [sb("tF%d" % i, [128, 512], F32) for i in range(4)]
        tB = [sb("tB%d" % i, [128, 512], BF16) for i in range(2)]
        qkn = [sb("qkn%d" % i, [128, 512], F32) for i in range(2)]
        rstd = [sb("rstd%d" % i, [128, 512], F32) for i in range(1)]
        stt = [sb("stt%d" % i, [128, 64], F32) for i in range(2)]
        PF = sb("PF_s", [128, NF], F32)
        wsT = sb("wsT_s", [128, 4, 128], BF16)
        modS2 = [sb("modS%d" % i, [128, 48, 2], F32) for i in range(2)]
        modD2 = [sb("modD%d" % i, [128, 2, 8, 2], F32) for i in range(2)]
        PA2 = [sb("PA%d" % i, [128, 64], F32) for i in range(2)]
        lamt = sb("lamt_s", [128, 40], F32)
        ident = sb("ident_s", [128, 128], BF16)
        onesb = sb("onesb_s", [128, 128], BF16)
        onesf = sb("onesf_s", [128, 128], F32)
        csc = sb("csc_s", [128, 256], BF16)
        dfb = sb("dfb_s", [128, 2, 2, 256], BF16)
        cs = sb("cs_s", [128, NTT, 32], F32)
        indt = sb("indt_s", [128, 12, 2, 8], BF16)
        flag = sb("flag_s", [128, 1], F32)
        condL = sb("condL_s", [128, 8, 2], F32)
        sT = sb("sT_s", [128, 8, 2], BF16)
        Y1 = QT[:, :, :].rearrange("p a b -> p (a b)").rearrange("p (t c n) -> p t c n", t=NTT, c=2)
        qtf = QT[:, :, :].rearrange("p a b -> p (a b)").bitcast(F32)
        vst = [qtf[:, r_ * 256:(r_ + 1) * 256] for r_ in range(2)]
        gvt = [qtf[:, 512 + r_ * 256:512 + (r_ + 1) * 256] for r_ in range(2)]
        ropet = [[qtf[:, 1024 + (2 * r_ + i_) * 256:1024 + (2 * r_ + i_ + 1) * 256] for i_ in range(2)]
                 for r_ in range(2)]
        bank = [st.enter_context(nc.psum_tensor("bank%d" % i, [128, 512], F32)) for i in range(8)]

        def B2(name, n, m):
            return [[Buf("%s_%d_%d" % (name, i, j)) for j in range(m)] for i in range(n)]

        def B1(name, n):
            return [Buf("%s_%d" % (name, i)) for i in range(n)]

        xTb = B2("xT", 8, 3)
        hTb = B2("hT", 8, 3)
        bigb = B2("big", 8, 3)
        slotb = B1("slot", NSLOT)
        bankb = B1("bank", 8)
        QTt = B1("QT", NTT)
        QmTb = B1("QmT", NTT)
        KTt = B1("KT", 12)
        Vt = B1("V", 12)
        qkaugb = B1("qkaug", 2)
        ctxab = Buf("ctxa")
        uGb = B2("uG", 2, 3)
        hcpb = B2("hcp", 2, 3)
        caccb = Buf("cacc")
        zdTb = B2("zdT", 2, 3)
        vnrb, Erb, tFb, tBb = B1("vnr", 2), B1("Er", 4), B1("tF", 4), B1("tB", 2)
        qknb, vstb, gvtb, ropetb, rstdb, sttb = (B1("qkn", 2), B1("vst", 2), B1("gvt", 2),
                                                 B2("ropet", 2, 2), B1("rstd", 1), B1("stt", 2))
        PFb, wsTb, lamb = Buf("PF"), Buf("wsT"), Buf("lam")
        modSb2, modDb2, PAb2 = B1("modS", 2), B1("modD", 2), B1("PA", 2)
        constb = Buf("const")
        cvtb, dummyb, cvt2b = Buf("cvt"), Buf("dummy"), Buf("cvt2")
        dgb = B1("dg", 8)
        sTb = Buf("sT")

        slot_g = [S.dgrp("slot%d" % i) for i in range(NSLOT)]
        g_const = S.dgrp("const")
        g_x = S.dgrp("xload")
        g_pf = S.dgrp("pf")
        g_pa = [S.dgrp("pa%d" % i) for i in range(2)]
        g_ws = S.dgrp("ws")
        g_y = S.dgrp("yout")
        g_kst = [S.dgrp("kst%d" % i) for i in range(2)]
        g_vst = [S.dgrp("vst%d" % i) for i in range(2)]
        g_dbg = S.dgrp("dbg")

        tFr = Ring(list(zip(tF, tFb)))
        tBr = Ring(list(zip(tB, tBb)))
        Err = Ring(list(zip(Er, Erb)))
        rstdr = Ring(list(zip(rstd, rstdb)))

        ring_state = {"i": 0}

        def load_slot(src_ap, cast=True):
            i = ring_state["i"] % NSLOT
            ring_state["i"] += 1
            t, b, g = ring[i], slotb[i], slot_g[i]
            shp = src_ap.shape
            dst = t[:, :].rearrange("p (a b) -> p a b", a=shp[1])
            S.add("pool", lambda e, dst=dst, src=src_ap: e.dma_start(out=dst, in_=src),
                  writes=[b], dgrp=g)
            return t, b

        def wslot_cols(w_l, c0):
            return w_l[:, c0:c0 + 512].rearrange("(k p) n -> p k n", p=128)

        def wslot_rows(w_l, r0):
            return w_l[r0:r0 + 512, :].rearrange("(f p) n -> p f n", p=128)

        def mm_group(out_ap, pairs, reads, writes):
            n = len(pairs)

            def fn(e, out_ap=out_ap, pairs=pairs, n=n):
                ins = None
                for i, (l, r) in enumerate(pairs):
                    ins = e.matmul(out_ap, l, r, start=(i == 0), stop=(i == n - 1))
                return ins
            return S.add("pe", fn, reads=reads, writes=writes)

        def act(out, in_, func, reads, writes, **kw):
            return S.add("act", lambda e: e.activation(out=out, in_=in_, func=func, **kw),
                         reads=reads, writes=writes)

        def dve_tt(out, in0, in1, op, reads, writes, eng="dve"):
            return S.add(eng, lambda e: e.tensor_tensor(out=out, in0=in0, in1=in1, op=op),
                         reads=reads, writes=writes)

        def dve_ts(out, in0, s1, s2, op0, op1, reads, writes, eng="dve"):
            if op1 is None:
                return S.add(eng, lambda e: e.tensor_scalar(out=out, in0=in0, scalar1=s1, scalar2=None,
                                                            op0=op0), reads=reads, writes=writes)
            return S.add(eng, lambda e: e.tensor_scalar(out=out, in0=in0, scalar1=s1, scalar2=s2,
                                                        op0=op0, op1=op1), reads=reads, writes=writes)

        def dve_stt(out, in0, scalar, in1, op0, op1, reads, writes):
            return S.add("dve", lambda e: e.scalar_tensor_tensor(out=out, in0=in0, scalar=scalar, in1=in1,
                                                                 op0=op0, op1=op1),
                         reads=reads, writes=writes)

        def copy(eng, out, in_, reads, writes):
            if eng == "act":
                return S.add("act", lambda e: e.activation(out=out, in_=in_, func=AF.Copy),
                             reads=reads, writes=writes)
            return S.add(eng, lambda e: e.tensor_copy(out=out, in_=in_), reads=reads, writes=writes)

        def rstd_op(out, in_, scale, reads_in, tmp, tmpb, outb):
            act(tmp, in_, AF.Ln, reads_in, [tmpb], scale=scale, bias=eps_ap(in_))
            act(out, tmp, AF.Exp, [tmpb], [outb], scale=-0.5)

        epsb = sb("eps_s", [128, 1], F32)
        dummy = sb("dummy_s", [128, 2], F32)
        cmask = sb("cmask_s", [128, 2], F32)
        dg8 = sb("dg8_s", [128, 8 * 128], BF16)

        def eps_ap(in_):
            p0 = in_.start_partition()
            n = in_.partition_size()
            return epsb[p0:p0 + n, 0:1]

        def dbg_dump(name, src_ap, shape, reads, dt=F32):
            if not debug:
                return
            d = dout("dbg_" + name, shape, dt)
            dbg_d[name] = d
            S.add("sp", lambda e: e.dma_start(out=d, in_=src_ap), reads=reads, dgrp=g_dbg)

        def cload(dst, src):
            S.add("sp", lambda e: e.dma_start(out=dst, in_=src), writes=[constb], dgrp=g_const)

        cload(ident[:, :], ident_d[:, :])
        cload(csc[:, :], csc_d[:, :])
        cload(dfb[:, :, :, :], dfb_d[:, :, :, :])
        cload(cs[:, :, :], cs_d[:, :, :])
        cload(indt[:, :, :, :], ind_d[:, :, :, :])
        cload(flag[:, :], flag_d[:, :])
        cload(condL[:, :, :], cond_d[:, :, :])
        for k in range(8):
            S.add("sp", lambda e, k=k: e.dma_start(out=xT[:, k, :], in_=xT_d[k * 128:(k + 1) * 128, :]),
                  writes=xTb[k], dgrp=g_x)
        S.add("pool", lambda e: e.memset(onesb[:, :], 1.0), writes=[constb])
        S.add("pool", lambda e: e.memset(onesf[:, :], 1.0), writes=[constb])
        S.add("pool", lambda e: e.memset(epsb[:, :], EPS), writes=[constb])
        S.add("pool", lambda e: e.memset(cmask[:, :], 0.0), writes=[constb])
        S.add("pool", lambda e: e.memset(cmask[0:64, 0:1], 1.0), writes=[constb])
        S.add("pool", lambda e: e.memset(cmask[64:128, 1:2], 1.0), writes=[constb])
        S.add("pool", lambda e: e.memset(Vaug[:, :, :, :], 1.0), writes=Vt)
        for r in range(2):
            S.add("pool", lambda e, r=r: e.memset(qkaug[r][:, :, :], 0.0), writes=[qkaugb[r]])
        S.add("pool", lambda e: e.memset(ctxa[:, :, :], 0.0), writes=[ctxab])
        S.add("pool", lambda e: e.memset(hcpad[:, :, :, :], 0.0), writes=hcpb[0] + hcpb[1])
        act(sT[:, :, :], condL[:, :, :], AF.Silu, [constb], [sTb])

        def chk(name):
            if stop == name:
                raise _Stop()

        def mod_gen(l, first=0, last=12):
            ms = l % 2
            modS, modD, modSb, modDb, PA, PAb = modS2[ms], modD2[ms], modSb2[ms], modDb2[ms], PA2[ms], PAb2[ms]
            if first == 0:
                S.add("sp", lambda e: e.dma_start(out=PA[:, :], in_=pa_d[l, :, :]), writes=[PAb], dgrp=g_pa[ms])
            pm, pmb = bank[7], bankb[7]
            slots = {}
            for s_ in range(first, min(first + 3, last)):
                slots[s_] = load_slot(wslot_cols(w_ada[l], s_ * 512))
                yield
            for s_ in range(first, last):
                t, b = slots[s_]
                tv = t[:, :].rearrange("p (k n) -> p k n", k=8)
                for j in range(4):
                    mm_group(pm[:, 2 * j:2 * j + 2],
                             [(tv[:, k, j * 128:(j + 1) * 128], sT[:, k, :]) for k in range(8)],
                             reads=[b, sTb], writes=[pmb])
                dve_tt(modS[:, 4 * s_:4 * s_ + 4, :], pm[:, 0:8].rearrange("p (n c) -> p n c", c=2),
                       bcast(PA[:, 4 * s_:4 * s_ + 4], 2, 2), ALU.add, [pmb, PAb], [modSb])
                if s_ + 3 < last:
                    slots[s_ + 3] = load_slot(wslot_cols(w_ada[l], (s_ + 3) * 512))
                if s_ == 3:
                    dve_stt(modD[:, 0, :, :], modS[:, 8:16, :], 1.0, bcast(PA[:, 48:56], 2, 2), ALU.add, ALU.mult,
                            [modSb, PAb], [modDb])
                if s_ == 9:
                    dve_stt(modD[:, 1, :, :], modS[:, 32:40, :], 1.0, bcast(PA[:, 56:64], 2, 2), ALU.add, ALU.mult,
                            [modSb, PAb], [modDb])
                yield

        def layer(l):
            lam_init = 0.8 - 0.6 * math.exp(-0.3 * l)
            S.add("sp", lambda e: e.dma_start(out=PF[:, :], in_=pf_d[l, :, :]), writes=[PFb], dgrp=g_pf)
            S.add("pool", lambda e: e.dma_start(out=wsT[:, :, :],
                                                in_=wsT_d[l, :, :].rearrange("p (g q) -> p g q", g=4)),
                  writes=[wsTb], dgrp=g_ws)
            ckst = qkn[0][:, :].rearrange("p (t n) -> p t n", t=2)
            cvst = qkn[1][:, :].rearrange("p (t n) -> p t n", t=2)
            ckstb, cvstb = qknb[0], qknb[1]
            S.add("sp", lambda e: e.dma_start(out=ckst, in_=ck_d[l, :, :].rearrange("(t p) n -> p t n", p=128)),
                  writes=[ckstb], dgrp=g_kst[0])
            S.add("sp", lambda e: e.dma_start(out=cvst, in_=cv_d[l, :, :].rearrange("(t p) n -> p t n", p=128)),
                  writes=[cvstb], dgrp=g_kst[1])

            lq1 = PF[:, O_LAM:O_LAM + 32]
            lk1 = PF[:, O_LAM + 32:O_LAM + 64]
            lq2 = PF[:, O_LAM + 64:O_LAM + 96]
            lk2 = PF[:, O_LAM + 96:O_LAM + 128]
            dve_tt(lamt[:, 0:32], lq1, lk1, ALU.mult, [PFb], [lamb])
            S.add("dve", lambda e: e.tensor_reduce(out=lamt[:, 32:33], in_=lamt[:, 0:32], axis=AX.X, op=ALU.add),
                  reads=[lamb], writes=[lamb])
            dve_tt(lamt[:, 0:32], lq2, lk2, ALU.mult, [PFb, lamb], [lamb])
            S.add("dve", lambda e: e.tensor_reduce(out=lamt[:, 33:34], in_=lamt[:, 0:32], axis=AX.X, op=ALU.add),
                  reads=[lamb], writes=[lamb])
            act(lamt[:, 34:36], lamt[:, 32:34], AF.Exp, [lamb], [lamb])
            dve_tt(lamt[:, 36:37], lamt[:, 35:36], lamt[:, 34:35], ALU.subtract, [lamb], [lamb])
            dve_ts(lamt[:, 36:37], lamt[:, 36:37], -lam_init, None, ALU.add, None, [lamb], [lamb])
            dve_ts(lamt[:, 37:38], PF[:, O_GHEAD:O_GHEAD + 1], 1.0 - lam_init, None, ALU.mult, None,
                   [PFb, lamb], [lamb])
            neglam = lamt[:, 36:37]
            ghs = lamt[:, 37:38]

            chk('lam')
            ms = l % 2
            modS, modD, modSb, modDb = modS2[ms], modD2[ms], modSb2[ms], modDb2[ms]
            if l == 0:
                for _ in mod_gen(0, 0, 4):
                    pass

            chk('mod')

            def mod_ap(split, k, c):
                return modS[:, split * 8 + k, c:c + 1]

            def norm_blk(which, bi, lyr):
                mS, mD, mSb, mDb = modS2[lyr % 2], modD2[lyr % 2], modSb2[lyr % 2], modDb2[lyr % 2]
                gidx = 0 if which == 1 else 1
                shs = 0 if which == 1 else 3
                t0, n = BLK[bi]
                c = 0 if bi < 2 else 1
                pb, pbb = bank[bi], bankb[bi]
                for k in range(8):
                    sq, sqb = tBr.next()
                    act(sq[:, 0:n], xT[:, k, t0:t0 + n], AF.Square, [xTb[k][bi]], [sqb])
                    S.add("pe", (lambda e, pb=pb, sq=sq, n=n, k=k:
                                 e.matmul(pb[:, 0:n], onesb[:, :], sq[:, 0:n], start=(k == 0), stop=(k == 7))),
                          reads=[sqb, constb], writes=[pbb])
                rs, rsb = rstdr.next()
                tmp, tmpb = tFr.next()
                rstd_op(rs[:, 0:n], pb[:, 0:n], 1.0 / D, [pbb, constb], tmp[:, 0:n], tmpb, rsb)
                for k in range(8):
                    tmp, tmpb = tFr.next()
                    dve_stt(tmp[:, 0:n], xT[:, k, t0:t0 + n], mD[:, gidx, k, c:c + 1], rs[:, 0:n],
                            ALU.mult, ALU.mult, [xTb[k][bi], mDb, rsb], [tmpb])
                    act(hT[:, k, t0:t0 + n], tmp[:, 0:n], AF.Identity, [tmpb, mSb], [hTb[k][bi]],
                        bias=mS[:, shs * 8 + k, c:c + 1])

            if l == 0:
                for bi_ in range(3):
                    norm_blk(1, bi_, 0)
            if debug and l == 0:
                dbg_dump("hT", hT[:, :, :], [128, 8, NT], [b for r in hTb for b in r], BF16)
                dbg_dump("modS", modS[:, :, :], [128, 48, 2], [modSb])
            chk('norm1')

            wi = [load_slot(wslot_cols(w_in[l], c * 512)) for c in range(4)]
            wiv = [t[:, :].rearrange("p (k n) -> p k n", k=8) for (t, b) in wi]
            wib = [b for (t, b) in wi]
            hall = lambda bi: [hTb[k][bi] for k in range(8)]
            pbr = Ring(list(range(0, 7)))

            for bi, (t0, n) in enumerate(BLK):
                segs = (2 * bi, 2 * bi + 2) if bi < 2 else (4, 5)
                nseg = segs[1] - segs[0]
                for j in range(2):
                    p = pbr.next()
                    mm_group(bank[p][:, 0:n],
                             [(wiv[1][:, k, 256 + j * 128:256 + (j + 1) * 128], hT[:, k, t0:t0 + n]) for k in range(8)],
                             hall(bi) + [wib[1]], [bankb[p]])
                    act(uG[:, j, t0:t0 + n], bank[p][:, 0:n], AF.Gelu_apprx_tanh, [bankb[p]], [uGb[j][bi]])
                    chk('feat_u')
                    pa = pbr.next()
                    mm_group(bank[pa][:, 0:n],
                             [(wiv[2][:, k, 256 + j * 128:256 + (j + 1) * 128], hT[:, k, t0:t0 + n]) for k in range(8)],
                             hall(bi) + [wib[2]], [bankb[pa]])
                    pg = pbr.next()
                    mm_group(bank[pg][:, 0:n],
                             [(wiv[3][:, k, j * 128:(j + 1) * 128], hT[:, k, t0:t0 + n]) for k in range(8)],
                             hall(bi) + [wib[3]], [bankb[pg]])
                    chk('feat_mm')
                    sg, sgb = tFr.next()
                    act(sg[:, 0:n], bank[pg][:, 0:n], AF.Sigmoid, [bankb[pg]], [sgb])
                    dve_tt(hcpad[:, j, segs[0]:segs[1], 15:271],
                           bank[pa][:, 0:n].rearrange("p (s t) -> p s t", s=nseg),
                           sg[:, 0:n].rearrange("p (s t) -> p s t", s=nseg), ALU.mult,
                           [bankb[pa], sgb], [hcpb[j][bi]])
                    chk('feat_conv')
                    pz = pbr.next()
                    mm_group(bank[pz][:, 0:n],
                             [(wiv[3][:, k, 256 + j * 128:256 + (j + 1) * 128], hT[:, k, t0:t0 + n]) for k in range(8)],
                             hall(bi) + [wib[3]], [bankb[pz]])
                    copy("act", zdT[:, j, t0:t0 + n], bank[pz][:, 0:n], [bankb[pz]], [zdTb[j][bi]])

            chk('feat')
            for ct in range(2):
                S.add("pool", lambda e, ct=ct: e.tensor_copy(
                    out=ctxa[:, :, 0:32], in_=ckst[:, ct, :].rearrange("p (g d) -> p g d", g=8)),
                    reads=[ckstb], writes=[ctxab])
                S.add("pool", lambda e, ct=ct: e.tensor_copy(
                    out=ctxa[:, :, 32:40], in_=bcast(indt[:, 10 + ct, 1, :], 1, 8)),
                    reads=[constb], writes=[ctxab])
                p = pbr.next()
                pv = bank[p][:, :].bitcast(BF16)

                def tr_ctx(e, ct=ct, pv=pv):
                    ins = None
                    for h in range(4):
                        ins = e.transpose(out=pv[:, h * 128:(h + 1) * 128],
                                          in_=ctxa[:, 2 * h:2 * h + 2, :].rearrange("p a b -> p (a b)"),
                                          identity=ident[:, :])
                    return ins
                S.add("pe", tr_ctx, reads=[ctxab, constb], writes=[bankb[p]])
                copy("dve", KT[:, :, NT + ct * 128:NT + (ct + 1) * 128],
                     pv[:, 0:512].rearrange("p (h t) -> p h t", h=4), [bankb[p]], [KTt[10 + ct]])
                cvv = cvst[:, ct, :].rearrange("p (a b d) -> p a b d", a=2, b=2)
                va = Vaug[:, 10 + ct, :, :].rearrange("p (a b) d -> p a b d", a=2)
                S.add("pool", lambda e, cvv=cvv, va=va: e.tensor_copy(out=va[:, :, 0, 0:64], in_=cvv[:, :, 0, :]),
                      reads=[cvstb], writes=[Vt[10 + ct]])
                S.add("pool", lambda e, cvv=cvv, va=va: e.tensor_copy(out=va[:, :, 1, 64:128], in_=cvv[:, :, 1, :]),
                      reads=[cvstb], writes=[Vt[10 + ct]])

            chk('ctx')
            tmpall = vstb + gvtb + ropetb[0] + ropetb[1]
            S.add("dve", lambda e: e.memset(dummy[:, 1:2], 0.0), writes=QTt + tmpall + [dummyb])
            tst = {}

            def stageA(tt):
                st_ = tst.setdefault(tt, {})
                bi = min(tt // 4, 2)
                r = tt % 2
                c0 = tt * 128
                pA = 2 * (tt % 3)
                pB = 2 * (tt % 3) + 1
                mm_group(bank[pA][:, :], [(hT[:, k, c0:c0 + 128], wiv[0][:, k, :]) for k in range(8)],
                         hall(bi) + [wib[0]], [bankb[pA]])
                mm_group(bank[pB][:, 0:256], [(hT[:, k, c0:c0 + 128], wiv[1][:, k, 0:256]) for k in range(8)],
                         hall(bi) + [wib[1]], [bankb[pB]])
                mm_group(bank[pB][:, 256:512], [(hT[:, k, c0:c0 + 128], wiv[2][:, k, 0:256]) for k in range(8)],
                         hall(bi) + [wib[2]], [bankb[pB]])

                st_.update(bi=bi, r=r, c0=c0, pA=pA, pB=pB)

            def genB1(tt):
                st_ = tst[tt]
                bi, r, c0, pA, pB = st_["bi"], st_["r"], st_["c0"], st_["pA"], st_["pB"]
                qn, qnb = qkn[r], qknb[r]
                act(qn[:, :], bank[pA][:, :], AF.Square, [bankb[pA]], [qnb])
                yield
                sv, svb = stt[r], sttb[r]
                S.add("dve", lambda e, qn=qn, sv=sv: e.tensor_reduce(
                    out=sv[:, 0:16], in_=qn[:, :].rearrange("p (g d) -> p g d", g=16), axis=AX.X, op=ALU.add),
                    reads=[qnb], writes=[svb])
                yield
                rstd_op(sv[:, 32:48], sv[:, 0:16], 1.0 / 32, [svb, constb], sv[:, 16:32], svb, svb)
                yield
                dve_tt(qn[:, :].rearrange("p (g d) -> p g d", g=16),
                       bank[pA][:, :].rearrange("p (g d) -> p g d", g=16),
                       bcast(sv[:, 32:48], 2, 32), ALU.mult, [bankb[pA], svb, qnb], [qnb])
                yield
                gqk = PF[:, O_GQK:O_GQK + 64].rearrange("p (a d) -> p a d", a=2)
                dve_tt(qn[:, :].rearrange("p (a g d) -> p a g d", a=2, g=8),
                       qn[:, :].rearrange("p (a g d) -> p a g d", a=2, g=8),
                       bcast(gqk, 2, 8), ALU.mult, [qnb, PFb], [qnb])
                yield
                S.add("sp", lambda e, qn=qn, c0=c0: e.dma_start(out=nk_d[l, c0:c0 + 128, :], in_=qn[:, 256:512]),
                      reads=[qnb], dgrp=g_kst[r])
                yield
                qv = qn[:, :].rearrange("p (g x h f) -> p g x h f", g=16, x=2, h=2)
                av, bv = qv[:, :, :, 0, :], qv[:, :, :, 1, :]
                cosv = bcast(cs[:, tt, 0:16].rearrange("p (x f) -> p x f", x=2), 1, 16)
                sinv = bcast(cs[:, tt, 16:32].rearrange("p (x f) -> p x f", x=2), 1, 16)
                qa, qab = qkaug[r], qkaugb[r]
                ov = qa[:, :, 0:32].rearrange("p g (x h f) -> p g x h f", x=2, h=2)
                oa, ob = ov[:, :, :, 0, :], ov[:, :, :, 1, :]
                r1 = ropet[r][0].rearrange("p (g x f) -> p g x f", g=16, x=2)
                r2 = ropet[r][1].rearrange("p (g x f) -> p g x f", g=16, x=2)
                ropetb_ = ropetb[r]
                P = "pool"
                dve_tt(r1, av, cosv, ALU.mult, [qnb, constb], [ropetb_[0]], eng=P)
                yield
                dve_tt(r2, bv, sinv, ALU.mult, [qnb, constb], [ropetb_[1]], eng=P)
                yield
                dve_tt(oa, r1, r2, ALU.subtract, [ropetb_[0], ropetb_[1]], [qab], eng=P)
                yield
                dve_tt(r1, bv, cosv, ALU.mult, [qnb, constb], [ropetb_[0]], eng=P)
                yield
                dve_tt(r2, av, sinv, ALU.mult, [qnb, constb], [ropetb_[1]], eng=P)
                yield
                dve_tt(ob, r1, r2, ALU.add, [ropetb_[0], ropetb_[1]], [qab], eng=P)
                yield
                S.add("pool", lambda e, qa=qa, tt=tt: e.tensor_copy(
                    out=qa[:, :, 32:40].rearrange("p (a g) c -> p a g c", a=2),
                    in_=bcast(indt[:, tt, :, :], 2, 8)), reads=[constb], writes=[qab])
                yield
                st_.update(qa=qa, qab=qab)

            def genB2(tt):
                st_ = tst[tt]
                bi, r, c0, pA, pB = st_["bi"], st_["r"], st_["c0"], st_["pA"], st_["pB"]
                sv, svb = stt[r], sttb[r]
                vs, vsb = vst[r], vstb[r]
                copy("act", vs, bank[pB][:, 0:256], [bankb[pB]], [vsb])
                yield
                S.add("sp", lambda e, vs=vs, c0=c0: e.dma_start(out=nv_d[l, c0:c0 + 128, :], in_=vs),
                      reads=[vsb], dgrp=g_vst[r])
                yield
                vv = vs.rearrange("p (a b d) -> p a b d", a=2, b=2)
                va = Vaug[:, tt, :, :].rearrange("p (a b) d -> p a b d", a=2)
                S.add("pool", lambda e, vv=vv, va=va: e.tensor_copy(out=va[:, :, 0, 0:64], in_=vv[:, :, 0, :]),
                      reads=[vsb], writes=[Vt[tt]])
                yield
                S.add("pool", lambda e, vv=vv, va=va: e.tensor_copy(out=va[:, :, 1, 64:128], in_=vv[:, :, 1, :]),
                      reads=[vsb], writes=[Vt[tt]])
                yield
                gv, gvb = gvt[r], gvtb[r]
                S.add("act", lambda e, gv=gv, pB=pB, sv=sv: e.activation(
                    out=gv, in_=bank[pB][:, 256:512], func=AF.Gelu_apprx_tanh, accum_out=sv[:, 48:49]),
                    reads=[bankb[pB]], writes=[gvb, svb])
                yield
                S.add("act", lambda e, gv=gv, pB=pB, sv=sv: e.activation(
                    out=bank[pB][:, 256:512], in_=gv, func=AF.Square, accum_out=sv[:, 49:50]),
                    reads=[gvb], writes=[bankb[pB], svb])
                yield
                dve_ts(sv[:, 50:51], sv[:, 48:49], 1.0 / 256, None, ALU.mult, None, [svb], [svb])
                yield
                dve_tt(sv[:, 51:52], sv[:, 50:51], sv[:, 50:51], ALU.mult, [svb], [svb])
                yield
                dve_stt(sv[:, 52:53], sv[:, 49:50], 1.0 / 256, sv[:, 51:52], ALU.mult, ALU.subtract,
                        [svb], [svb])
                yield
                rstd_op(sv[:, 54:55], sv[:, 52:53], 1.0, [svb, constb], sv[:, 53:54], svb, svb)
                yield
                dve_ts(gv, gv, sv[:, 50:51], sv[:, 54:55], ALU.subtract, ALU.mult,
                       [gvb, svb], [gvb])
                yield
                dve_tt(gv, gv, PF[:, O_GSG:O_GSG + 256], ALU.mult, [gvb, PFb], [gvb])
                yield
                vn, vnb = vnr[r], vnrb[r]
                dve_tt(vn[:, :], gv, PF[:, O_BSG:O_BSG + 256], ALU.add, [gvb, PFb], [vnb])
                yield

                st_.update(vn=vn, vnb=vnb)

            def stageC(tt):
                st_ = tst[tt]
                bi, r, c0 = st_["bi"], st_["r"], st_["c0"]
                qa, qab, vn, vnb = st_["qa"], st_["qab"], st_["vn"], st_["vnb"]
                pT = 6
                pv = bank[pT][:, :].bitcast(BF16)

                def tr_qk(e, qa=qa, pv=pv):
                    ins = None
                    for j in range(8):
                        ins = e.transpose(out=pv[:, j * 128:(j + 1) * 128],
                                          in_=qa[:, 2 * j:2 * j + 2, :].rearrange("p a b -> p (a b)"),
                                          identity=ident[:, :])
                    return ins
                S.add("pe", tr_qk, reads=[qab, constb], writes=[bankb[pT]])
                for c in range(2):
                    dve_ts(hT[:, 4 * c:4 * c + 4, c0:c0 + 128], pv[:, 0:512].rearrange("p (h t) -> p h t", h=4),
                           cmask[:, c:c + 1], None, ALU.mult, None, [bankb[pT], constb], [QmTb[tt]])
                copy("dve", KT[:, :, c0:c0 + 128], pv[:, 512:1024].rearrange("p (h t) -> p h t", h=4),
                     [bankb[pT]], [KTt[tt]])

                pG = 7
                for ch in range(2):
                    for gi in range(2):
                        g = 2 * ch + gi
                        mm_group(bank[pG][gi * 64:(gi + 1) * 64, ch * 128:(ch + 1) * 128],
                                 [(vn[:, g * 64:(g + 1) * 64], wsT[:, g, :])],
                                 [vnb, wsTb], [bankb[pG]])
                tg, tgb = tFr.next()
                dve_tt(tg[:, 0:256], bank[pG][:, 0:256], PF[:, O_BSP:O_BSP + 256], ALU.add,
                       [bankb[pG], PFb], [tgb])
                dve_tt(bigT[:, 2:4, c0:c0 + 128], tg[:, 0:256].rearrange("p (c t) -> p c t", c=2),
                       uG[:, :, c0:c0 + 128], ALU.mult, [tgb, uGb[0][bi], uGb[1][bi]],
                       [bigb[2][bi], bigb[3][bi]])


            def run_interleaved(gens):
                gens = list(gens)
                while gens:
                    for g_ in list(gens):
                        if next(g_, "done") == "done":
                            gens.remove(g_)

            for s_ in range(NTT + 3):
                if s_ < NTT:
                    stageA(s_)
                if 0 <= s_ - 3 < NTT:
                    stageC(s_ - 3)
                gl = []
                if 0 <= s_ - 1 < NTT:
                    gl.append(genB1(s_ - 1))
                if 0 <= s_ - 2 < NTT:
                    gl.append(genB2(s_ - 2))
                run_interleaved(gl)

            chk('tok')
            def conv_gen():
                uflat = uG[:, :, :].rearrange("p a b -> p (a b)").bitcast(F32)
                sqT, meanT = uflat[:, 0:512], uflat[:, 512:1024]
                dgall = dg8[:, :]
                alluG = uGb[0] + uGb[1]
                S.add("dve", lambda e: e.memset(dummy[:, 0:1], 0.0), writes=alluG + [cvtb, cvt2b, dummyb])
                for j in range(2):
                    hb = hcpb[j]
                    dve_ts(hcpad[:, j, 1:4, 0:15], hcpad[:, j, 0:3, 256:271], flag[:, 0:1], None, ALU.mult, None,
                           hb + [constb], hb)
                    dve_ts(hcpad[:, j, 0:3, 271:286], hcpad[:, j, 1:4, 15:30], flag[:, 0:1], None, ALU.mult, None,
                           hb + [constb], hb)
                    yield
                di = 0
                lnstate = {"done": -100, "last_evac": -100}
                p7, p7b = bank[7], bankb[7]
                for ui, (bi, j) in enumerate([(b_, j_) for b_ in range(3) for j_ in range(2)]):
                    t0, n = BLK[bi]
                    segs = (2 * bi, 2 * bi + 2) if bi < 2 else (4, 5)
                    cb = 2 + (ui % 2)
                    AH = 6
                    for step in range(31 + AH):
                        if step < 31:
                            k = step
                            sl = (di + k) % 8
                            dg = dgall[:, sl * 128:(sl + 1) * 128]
                            wk = PF[:, O_WDW + j * 31 + k:O_WDW + j * 31 + k + 1]
                            S.add("dve", lambda e, dg=dg, wk=wk: e.tensor_scalar(
                                out=dg, in0=ident[:, :], scalar1=wk, scalar2=None, op0=ALU.mult),
                                reads=[constb, PFb], writes=[dgb[sl]])
                        if step >= AH:
                            k = step - AH
                            sl = (di + k) % 8
                            dg = dgall[:, sl * 128:(sl + 1) * 128]
                            S.add("pe", (lambda e, cb=cb, n=n, dg=dg, j=j, segs=segs, k=k: e.matmul(
                                bank[cb][:, 0:n], dg, hcpad[:, j, segs[0]:segs[1], k:k + 256],
                                start=(k == 0), stop=(k == 30))),
                                reads=[dgb[sl], hcpb[j][bi]], writes=[bankb[cb]])
                        yield
                    di += 31

                    evp[0] += 1

                    def evac(j=j, cb=cb, n=n):
                        evp[0] -= 1
                        dve_ts(cacc[:, j, 0:n], bank[cb][:, 0:n], PF[:, O_BDW + j:O_BDW + j + 1], None,
                               ALU.add, None, [bankb[cb], PFb], [caccb])
                    cdef.append((max(cur["i"] + 8, lnstate["done"] + 2), evac))
                    lnstate["last_evac"] = max(cur["i"] + 8, lnstate["done"] + 2)
                    if j == 0:
                        continue
                    sqA = sqT.bitcast(BF16)[:, 0:512]
                    sqB = sqT.bitcast(BF16)[:, 512:1024]

                    def s0(n=n):
                        mm_group(p7[:, 0:n], [(onesf[:, :], cacc[:, jj, 0:n]) for jj in range(2)],
                                 [caccb, constb], [p7b])
                        dve_ts(meanT[:, 0:n], p7[:, 0:n], 1.0 / 256, None, ALU.mult, None, [p7b], [cvtb])

                    def s1(n=n):
                        for jj, sq_ in ((0, sqA), (1, sqB)):
                            dve_tt(cacc[:, jj, 0:n], cacc[:, jj, 0:n], meanT[:, 0:n], ALU.subtract,
                                   [caccb, cvtb], [caccb], eng="pool")
                            dve_tt(sq_[:, 0:n], cacc[:, jj, 0:n], cacc[:, jj, 0:n], ALU.mult, [caccb], [cvt2b],
                                   eng="pool")

                    def s2(n=n):
                        mm_group(p7[:, 0:n], [(onesb[:, :], sqA[:, 0:n]), (onesb[:, :], sqB[:, 0:n])],
                                 [cvt2b, constb], [p7b])
                        act(meanT[:, 0:n], p7[:, 0:n], AF.Ln, [p7b, constb], [cvtb], scale=1.0 / 256,
                            bias=epsb[:, 0:1])

                    def s3(n=n):
                        act(meanT[:, 0:n], meanT[:, 0:n], AF.Exp, [cvtb], [cvtb], scale=-0.5)

                    def s4(n=n):
                        for jj in range(2):
                            dve_tt(cacc[:, jj, 0:n], cacc[:, jj, 0:n], meanT[:, 0:n], ALU.mult, [caccb, cvtb], [caccb],
                                   eng="pool")

                    def s5(n=n, t0=t0, bi=bi):
                        for jj in range(2):
                            S.add("act", lambda e, jj=jj: e.activation(
                                out=bigT[:, 4 + jj, t0:t0 + n], in_=cacc[:, jj, 0:n], func=AF.Silu,
                                scale=PF[:, O_GCV + jj:O_GCV + jj + 1], bias=PF[:, O_BCV + jj:O_BCV + jj + 1]),
                                reads=[caccb, PFb], writes=[bigb[4 + jj][bi]])
                    base_ = lnstate["last_evac"] + 4
                    for si, st_fn in enumerate((s0, s1, s2, s3, s4, s5)):
                        cdef.append((base_ + 4 * si + (4 if si >= 2 else 0), st_fn))
                    lnstate["done"] = base_ + 4 * 5 + 4
                S.add("dve", lambda e: e.memset(dummy[:, 0:1], 0.0), writes=[cvtb, cvt2b] + alluG + [dummyb])
                yield

            groupsA, groupsB = [], []
            for h in range(4):
                for qb in range(2):
                    groupsA.append((h, qb * 512, 512, [10, 11] + list(range(8)), [4 * qb + i for i in range(4)], qb))
            for h in range(4):
                groupsB.append((h, 1024, 256, [8, 9], [8, 9], 2))
            groups = []
            for i_ in range(8):
                groups.append(groupsA[i_])
                if i_ < 4:
                    groups.append(groupsB[i_])
            Sbank = Ring([4, 5, 6])
            Oset = Ring([(0, 1)])
            items = []
            for gi, (h, q0, nq, kts, qtiles, bi) in enumerate(groups):
                ob = Oset.next()
                for c in range(2):
                    for ki, kt in enumerate(kts):
                        items.append((gi, c, ki, kt, ob))

            def kcols(kt):
                return (NT + (kt - 10) * 128) if kt >= 10 else kt * 128

            pend = []

            def emit_S(it):
                gi, c, ki, kt, ob = it
                h, q0, nq, kts, qtiles, bi = groups[gi]
                sbk = Sbank.next()
                k0 = kcols(kt)
                if DUMMY_N:
                    S.add("pe", lambda e, sbk=sbk: e.matmul(bank[sbk][:, 0:DUMMY_N], ident[:, :], hT[:, 0, 0:DUMMY_N],
                                                            start=True, stop=True),
                          reads=[constb, hTb[0][0]], writes=[bankb[sbk]])
                mm_group(bank[sbk][:, 0:nq],
                         [(KT[:, h, k0:k0 + 128], hT[:, 4 * c + h, q0:q0 + nq])],
                         [KTt[kt], hTb[4 * c + h][bi]] + [QmTb[t] for t in qtiles], [bankb[sbk]])
                E, Eb = Err.next()
                act(E[:, 0:nq], bank[sbk][:, 0:nq], AF.Exp, [bankb[sbk]], [Eb], scale=QSCALE)
                pend.append((it, E, Eb))

            def emit_PV(it, E, Eb):
                gi, c, ki, kt, ob = it
                h, q0, nq, kts, qtiles, bi = groups[gi]
                last = (ki == len(kts) - 1)
                if ki == 0:
                    flush_O_readers(c)
                S.add("pe", lambda e: e.matmul(bank[ob[c]][:, 0:nq], Vaug[:, kt, h, :], E[:, 0:nq],
                                               start=(ki == 0), stop=last),
                      reads=[Vt[kt], Eb], writes=[bankb[ob[c]]])
                if last:
                    finalize(gi, ob, c)

            gq = []
            cur = {"i": 0}

            def live_T():
                return sum(g_["nT"] for g_ in gq)

            def run_group_until(g_, pred):
                while any(pred(st_) for st_ in g_["steps"]):
                    g_["steps"].pop(0)[2]()

            def flush_O_readers(c):
                for g_ in gq:
                    run_group_until(g_, lambda st_: st_[1] == c)

            def finalize(gi, ob, c):
                h, q0, nq, kts, qtiles, bi = groups[gi]
                nb = (h % 2) * 64
                db = 64 - nb
                nr = slice(nb, nb + 64)
                dr = slice(db, db + 64)
                Oc = bank[ob[c]]
                imm = False
                D = (lambda d: 0) if imm else (lambda d: d)
                t_ = cur["i"]
                nxt = tFr.i % len(tFr.items)
                for g0 in list(gq):
                    if nxt in g0["Tidx"]:
                        assert g0["done"], "T ring too small"
                        run_group_until(g0, lambda st_: True)
                        gq.remove(g0)
                T, Tb = tFr.next()
                if c == 0:
                    grp = {"gi": gi, "steps": [], "nT": 1, "done": False, "T0": (T, Tb), "Tidx": [nxt]}
                    gq.append(grp)
                else:
                    grp = [g_ for g_ in gq if g_["gi"] == gi][0]
                    grp["nT"] = 2
                    grp["done"] = True
                    grp["Tidx"].append(nxt)
                steps = grp["steps"]
                if c == 0 and not imm:
                    steps.append((t_ + D(1), c, lambda: S.add(
                        "dve", lambda e: e.reciprocal(out=T[nr, 0:nq], in_=Oc[dr, 0:nq]),
                        reads=[bankb[ob[c]]], writes=[Tb])))
                else:
                    steps.append((t_ + D(1), c,
                                  lambda: act(T[nr, 0:nq], Oc[dr, 0:nq], AF.Ln, [bankb[ob[c]]], [Tb])))
                    steps.append((t_ + D(3), None,
                                  lambda: act(T[nr, 0:nq], T[nr, 0:nq], AF.Exp, [Tb], [Tb], scale=-1.0)))
                steps.append((t_ + D(5), c,
                              lambda: dve_tt(T[nr, 0:nq], Oc[nr, 0:nq], T[nr, 0:nq], ALU.mult,
                                             [bankb[ob[c]], Tb], [Tb])))
                if c == 0:
                    return
                T0, T0b = grp["T0"]
                T1, T1b = T, Tb
                sq, sqb = tBr.next()

                def st_mix():
                    dve_stt(T0[nr, 0:nq], T1[nr, 0:nq], neglam[nr, :], T0[nr, 0:nq], ALU.mult, ALU.add,
                            [T0b, T1b, lamb], [T0b])
                    dve_tt(sq[nr, 0:nq], T0[nr, 0:nq], T0[nr, 0:nq], ALU.mult, [T0b], [sqb])

                def st_ss():
                    mm_group(bank[7][nr, 0:nq], [(onesb[nr, 0:64], sq[nr, 0:nq])], [sqb, constb], [bankb[7]])
                    act(T1[nr, 0:nq], bank[7][nr, 0:nq], AF.Ln, [bankb[7], constb], [T1b], scale=1.0 / 64,
                        bias=epsb[nr, 0:1])

                def st_exp():
                    act(T1[nr, 0:nq], T1[nr, 0:nq], AF.Exp, [T1b], [T1b], scale=-0.5)

                def st_out():
                    dve_stt(bigT[nr, h // 2, q0:q0 + nq], T0[nr, 0:nq], ghs[nr, :], T1[nr, 0:nq],
                            ALU.mult, ALU.mult, [T0b, T1b, lamb], [bigb[h // 2][bi]])
                steps.append((t_ + D(7), None, st_mix))
                steps.append((t_ + D(10), None, st_ss))
                steps.append((t_ + D(12), None, st_exp))
                steps.append((t_ + D(14), None, st_out))

            def run_deferred(force=False):
                for g_ in list(gq):
                    while g_["steps"] and (force or g_["steps"][0][0] <= cur["i"]):
                        g_["steps"].pop(0)[2]()
                    if g_["done"] and not g_["steps"]:
                        gq.remove(g_)
                    elif not force and g_["steps"]:
                        pass

            cdef = []
            evp = [0]

            def run_cdef(force=False):
                k_ = 0
                while k_ < len(cdef):
                    if force or cdef[k_][0] <= cur["i"]:
                        cdef.pop(k_)[1]()
                    else:
                        k_ += 1

            LOOK = 2
            def y1_gen():
                S.add("dve", lambda e: e.memset(dummy[:, 1:2], 0.0), writes=tmpall + QTt + [dummyb])
                for tt in range(NTT):
                    bi_ = min(tt // 4, 2)
                    p = 2 + (tt % 2)
                    for ch in range(2):
                        mm_group(bank[p][:, ch * 256:(ch + 1) * 256],
                                 [(zdT[:, ch, tt * 128:(tt + 1) * 128], csc[:, :])],
                                 [zdTb[ch][bi_], constb], [bankb[p]])
                    copy("dve", Y1[:, tt, :, :],
                         bank[p][:, :].rearrange("p (c n) -> p c n", c=2), [bankb[p]], QTt)
                    yield

            def mg_chain():
                if l == 0:
                    yield from mod_gen(0, 4, 12)
                if l + 1 < L:
                    yield from mod_gen(l + 1)
            mg = mg_chain()
            cg = conv_gen()

            def adv(g, n):
                for _ in range(n):
                    if next(g, "done") == "done":
                        return True
                return False

            yg = y1_gen()
            gstate = {"cg_done": False, "mg_done": False, "dsl": None}

            def issue_dsl():
                if gstate["dsl"] is None:
                    gstate["dsl"] = [load_slot(dfa_d[i_, :, :].rearrange("p (k n) -> p k n", k=8), cast=False)
                                     for i_ in range(4)]

            for i, it in enumerate(items):
                cur["i"] = i
                emit_S(it)
                if i >= LOOK:
                    emit_PV(*pend.pop(0))
                run_deferred()
                run_cdef()
                if not gstate["cg_done"]:
                    gstate["cg_done"] = adv(cg, 2 if i % 2 == 0 else 1)
                elif evp[0] == 0:
                    adv(yg, 1)
                if i % (6 if l == 0 else 10) == 5 and not gstate["mg_done"]:
                    gstate["mg_done"] = adv(mg, 1)
                    if gstate["mg_done"]:
                        issue_dsl()
            while pend:
                emit_PV(*pend.pop(0))
            run_deferred(force=True)
            adv(mg, 10000)
            for _ in range(400):
                cur["i"] += 1
                run_cdef()
                adv(cg, 2)
            run_cdef(force=True)
            adv(cg, 10000)
            run_cdef(force=True)
            issue_dsl()
            dsl_all = gstate["dsl"]
            adv(yg, 10000)

            chk('att')
            chk('conv')
            for th in range(2):
                dslots = dsl_all[2 * th:2 * th + 2]
                if th == 1:
                    wo = [load_slot(wslot_cols(w_out[l], c * 512)) for c in range(2)]
                cv_ = dslots[0][0][:, :].rearrange("p (k n) -> p k n", k=8)
                sv_ = dslots[1][0][:, :].rearrange("p (k n) -> p k n", k=8)
                for ch in range(2):
                    p = pbr.next()
                    pairs = [(Y1[:, tc, ch, 0:128], cv_[:, tc, :]) for tc in range(8)] + \
                            [(Y1[:, tc, ch, 128:256], sv_[:, tc, :]) for tc in range(8)]
                    mm_group(bank[p][:, :], pairs, QTt + [dslots[0][1], dslots[1][1]], [bankb[p]])
                    copy("act" if ch else "dve", bigT[:, 6 + ch, th * 512:(th + 1) * 512], bank[p][:, :],
                         [bankb[p]], [bigb[6 + ch][th]])
            for ch in range(2):
                p = pbr.next()
                pairs = [(Y1[:, 8 + tc, ch, 0:128], dfb[:, 0, tc, :]) for tc in range(2)] + \
                        [(Y1[:, 8 + tc, ch, 128:256], dfb[:, 1, tc, :]) for tc in range(2)]
                mm_group(bank[p][:, 0:256], pairs, QTt + [constb], [bankb[p]])
                copy("act" if ch else "dve", bigT[:, 6 + ch, 1024:1280], bank[p][:, 0:256],
                     [bankb[p]], [bigb[6 + ch][2]])

            if debug and l == 0:
                dbg_dump("mixcat", bigT[:, :, :], [128, 8, NT], [b for r in bigb for b in r], BF16)

            chk('fourier')
            pbr2 = Ring(list(range(3, 8)))
            w1n = [load_slot(wslot_cols(w_ff1[l], c * 512)) for c in range(2)]
            for bi, (t0, n) in enumerate(BLK):
                c = 0 if bi < 2 else 1
                for dt in range(8):
                    t, b = wo[dt // 4]
                    tv = t[:, :].rearrange("p (k n) -> p k n", k=8)
                    p = pbr2.next()
                    mm_group(bank[p][:, 0:n],
                             [(tv[:, k, (dt % 4) * 128:(dt % 4 + 1) * 128], bigT[:, k, t0:t0 + n]) for k in range(8)],
                             [bigb[k][bi] for k in range(8)] + [b], [bankb[p]])
                    dve_stt(xT[:, dt, t0:t0 + n], bank[p][:, 0:n], mod_ap(2, dt, c), xT[:, dt, t0:t0 + n],
                            ALU.mult, ALU.add, [bankb[p], modSb, xTb[dt][bi]], [xTb[dt][bi]])
                if bi >= 1:
                    norm_blk(2, bi - 1, l)
            w2n = [load_slot(wslot_rows(w_ff2[l], c * 512)) for c in range(2)]
            norm_blk(2, 2, l)
            if debug and l == 0:
                dbg_dump("x1", xT[:, :, :], [128, 8, NT], [b for r in xTb for b in r])

            chk('wout')
            for fq in range(4):
                w1 = w1n
                w2 = w2n if fq == 0 else [load_slot(wslot_rows(w_ff2[l], fq * 1024 + c * 512)) for c in range(2)]
                for fc in range(8):
                    t, b = w1[fc // 4]
                    tv = t[:, :].rearrange("p (k n) -> p k n", k=8)
                    for bi, (t0, n) in enumerate(BLK):
                        p = pbr2.next()
                        mm_group(bank[p][:, 0:n],
                                 [(tv[:, k, (fc % 4) * 128:(fc % 4 + 1) * 128], hT[:, k, t0:t0 + n]) for k in range(8)],
                                 hall(bi) + [b], [bankb[p]])
                        rb, rbb = tBr.next()
                        act(rb[:, 0:n], bank[p][:, 0:n], AF.Relu, [bankb[p]], [rbb])
                        dve_tt(bigT[:, fc, t0:t0 + n], rb[:, 0:n], rb[:, 0:n], ALU.mult, [rbb], [bigb[fc][bi]],
                               eng="pool")
                if fq < 3:
                    w1n = [load_slot(wslot_cols(w_ff1[l], (fq + 1) * 1024 + c * 512)) for c in range(2)]
                order = [(dt, bi) for dt in range(8) for bi in range(3)] if fq < 3 else \
                        [(dt, bi) for bi in range(3) for dt in range(8)]
                for (dt, bi) in order:
                    if True:
                        t0, n = BLK[bi]
                        c = 0 if bi < 2 else 1
                        p = pbr2.next()
                        pairs = []
                        for fc in range(8):
                            t, b = w2[fc // 4]
                            tv = t[:, :].rearrange("p (f n) -> p f n", f=4)
                            pairs.append((tv[:, fc % 4, dt * 128:(dt + 1) * 128], bigT[:, fc, t0:t0 + n]))
                        mm_group(bank[p][:, 0:n], pairs,
                                 [bigb[fc][bi] for fc in range(8)] + [w2[0][1], w2[1][1]], [bankb[p]])
                        dve_stt(xT[:, dt, t0:t0 + n], bank[p][:, 0:n], mod_ap(5, dt, c), xT[:, dt, t0:t0 + n],
                                ALU.mult, ALU.add, [bankb[p], modSb, xTb[dt][bi]], [xTb[dt][bi]])
                        if fq == 3 and dt == 7 and l + 1 < L and bi >= 1:
                            norm_blk(1, bi - 1, l + 1)
            if l + 1 < L:
                norm_blk(1, 2, l + 1)

        try:
            for l in range(L):
                layer(l)
        except _Stop:
            pass

        for k in range(8):
            S.add("sp", lambda e, k=k: e.dma_start(out=yT_d[k * 128:(k + 1) * 128, :], in_=xT[:, k, :]),
                  reads=xTb[k], dgrp=g_y)

        S.emit()
    return nc, dbg_d


def _bf(a):
    return np.ascontiguousarray(a).astype(ml_dtypes.bfloat16)


def _core_units(i):
    if i < 2:
        return True, [i], 24 + i
    return False, list(range(4 * (i - 2), 4 * (i - 2) + 4)), 24 + i


def _const_tables(is_sample):
    cs = np.zeros((NT, 32), np.float32)
    cs[:, 0:16] = 1.0
    if is_sample:
        pos = np.arange(1024)
        row = (pos // 64).astype(np.float32)
        col = (pos % 64).astype(np.float32)
        nf = 8
        inv = (10000.0 ** (-np.arange(nf, dtype=np.float32) / nf)).astype(np.float32)
        ang = np.stack([row[:, None] * inv, col[:, None] * inv], axis=1).astype(np.float32)
        cs[0:1024, 0:16] = np.cos(ang).reshape(1024, 16)
        cs[0:1024, 16:32] = np.sin(ang).reshape(1024, 16)
    cs = cs.reshape(NTT, 128, 32).transpose(1, 0, 2)
    ind = np.zeros((12, 128, 2, 8), np.float32)
    if not is_sample:
        for tt in range(8):
            seg = tt // 2
            ind[tt, :, 0, seg] = 1.0
            ind[tt, :, 1, 0:4] = -BIG
            ind[tt, :, 1, seg] = 0.0
        ind[10:12, :, 1, 0:4] = -BIG
    ind = ind.transpose(1, 0, 2, 3)
    flag = np.full((128, 1), 1.0 if is_sample else 0.0, np.float32)
    c = np.arange(64)
    ang = 2 * np.pi * np.outer(c, c) / 64.0
    csc = np.zeros((128, 256), np.float64)
    for g in range(2):
        csc[g * 64:(g + 1) * 64, g * 64:(g + 1) * 64] = np.cos(ang) / 8.0
        csc[g * 64:(g + 1) * 64, 128 + g * 64:128 + (g + 1) * 64] = np.sin(ang) / 8.0
    Lq = 1024 if is_sample else 256
    t = np.arange(Lq)
    a = 2 * np.pi * np.outer(t, t) / Lq
    Cq, Sq = np.cos(a) / math.sqrt(Lq), -np.sin(a) / math.sqrt(Lq)
    CA = np.zeros((1024, 1024))
    SA = np.zeros((1024, 1024))
    for s in range(1024 // Lq):
        CA[s * Lq:(s + 1) * Lq, s * Lq:(s + 1) * Lq] = Cq
        SA[s * Lq:(s + 1) * Lq, s * Lq:(s + 1) * Lq] = Sq
    dfa = np.zeros((4, 128, 8, 512))
    for th in range(2):
        dfa[2 * th + 0] = CA[:, th * 512:(th + 1) * 512].reshape(8, 128, 512).transpose(1, 0, 2)
        dfa[2 * th + 1] = SA[:, th * 512:(th + 1) * 512].reshape(8, 128, 512).transpose(1, 0, 2)
    dfa = dfa.reshape(4, 128, 4096)
    t = np.arange(256)
    a = 2 * np.pi * np.outer(t, t) / 256.0
    Cb, Sb = np.cos(a) / 16.0, -np.sin(a) / 16.0
    dfb = np.zeros((128, 2, 2, 256))
    dfb[:, 0] = Cb.reshape(2, 128, 256).transpose(1, 0, 2)
    dfb[:, 1] = Sb.reshape(2, 128, 256).transpose(1, 0, 2)
    return dict(cs=np.ascontiguousarray(cs), indt=_bf(ind), flag=flag, csc=_bf(csc), dfa=_bf(dfa),
                dfb=_bf(dfb), ident=_bf(np.eye(128)))


def _param_image(inp):
    pf = np.zeros((DEPTH, 128, NF), np.float32)
    p = np.arange(128)
    for l in range(DEPTH):
        pf[l, :, O_BADA:O_BADA + 48] = inp["b_ada"][l].reshape(48, 128).T
        pf[l, :, O_GATT:O_GATT + 8] = inp["g_attn_norm"][l].reshape(8, 128).T
        pf[l, :, O_GMLP:O_GMLP + 8] = inp["g_mlp_norm"][l].reshape(8, 128).T
        pf[l, :, O_GQK:O_GQK + 32] = inp["g_q"][l][None, :]
        pf[l, :, O_GQK + 32:O_GQK + 64] = inp["g_k"][l][None, :]
        pf[l, :, O_GSG:O_GSG + 256] = inp["g_sg"][l][None, :]
        pf[l, :, O_BSG:O_BSG + 256] = inp["b_sg"][l][None, :]
        pf[l, :, O_GHEAD] = inp["g_head"][l][p % 64]
        pf[l, :, O_LAM:O_LAM + 32] = inp["lam_q1"][l][None, :]
        pf[l, :, O_LAM + 32:O_LAM + 64] = inp["lam_k1"][l][None, :]
        pf[l, :, O_LAM + 64:O_LAM + 96] = inp["lam_q2"][l][None, :]
        pf[l, :, O_LAM + 96:O_LAM + 128] = inp["lam_k2"][l][None, :]
        for j in range(2):
            pf[l, :, O_WDW + j * 31:O_WDW + (j + 1) * 31] = inp["w_dw"][l][:, j * 128:(j + 1) * 128].T
            pf[l, :, O_BDW + j] = inp["b_dw"][l][j * 128:(j + 1) * 128]
            pf[l, :, O_GCV + j] = inp["g_conv"][l][j * 128:(j + 1) * 128]
            pf[l, :, O_BCV + j] = inp["b_conv"][l][j * 128:(j + 1) * 128]
            for gi in range(2):
                pf[l, gi * 64:(gi + 1) * 64, O_BSP + j * 128:O_BSP + (j + 1) * 128] = \
                    inp["b_spatial"][l][2 * j + gi][None, :]
    wsT = np.ascontiguousarray(np.transpose(inp["w_spatial"], (0, 3, 1, 2))).reshape(DEPTH, 128, 512)
    return pf, wsT


def make_in_maps(inp):
    inp = {k: np.asarray(v) for k, v in inp.items()}
    pf, wsT = _param_image(inp)
    pa = np.ascontiguousarray(pf[:, :, 0:64])
    shared = dict(pf=pf, pa=pa, wsT=wsT.astype(np.float32),
                  w_ada=np.ascontiguousarray(inp["w_ada"], dtype=np.float32),
                  w_in=np.ascontiguousarray(inp["w_in"], dtype=np.float32),
                  w_out=np.ascontiguousarray(inp["w_out"], dtype=np.float32),
                  w_ff1=np.ascontiguousarray(inp["w_ff1"], dtype=np.float32),
                  w_ff2=np.ascontiguousarray(inp["w_ff2"], dtype=np.float32))
    tabs = {True: _const_tables(True), False: _const_tables(False)}
    maps = []
    for i in range(8):
        is_s, seqs, bseq = _core_units(i)
        if is_s:
            xa = inp["x_sample"][seqs[0]]
            condA = inp["c"][seqs[0]]
            ck = inp["cache_k"][seqs[0]].reshape(DEPTH, 256, 256)
            cv = inp["cache_v"][seqs[0]].reshape(DEPTH, 256, 256)
        else:
            xa = inp["x_prompt"][seqs].reshape(1024, D)
            condA = inp["c_ctx"]
            ck = np.zeros((DEPTH, 256, 256), np.float32)
            cv = np.zeros((DEPTH, 256, 256), np.float32)
        x = np.concatenate([xa, inp["x_prompt"][bseq]], axis=0)
        cond = np.stack([condA, inp["c_ctx"]], axis=1)
        m = dict(shared)
        m.update(tabs[is_s])
        m["xT"] = np.ascontiguousarray(x.T, dtype=np.float32)
        m["condL"] = np.ascontiguousarray(cond.reshape(8, 128, 2).transpose(1, 0, 2), dtype=np.float32)
        m["ck"] = np.ascontiguousarray(ck, dtype=np.float32)
        m["cv"] = np.ascontiguousarray(cv, dtype=np.float32)
        maps.append(m)
    return maps


_NC_CACHE = {}


def kernel(**inputs):
    if "nc" not in _NC_CACHE:
        _NC_CACHE["nc"] = build()[0]
    nc = _NC_CACHE["nc"]
    maps = make_in_maps(inputs)
    res = run_bass_kernel_spmd(nc, maps, core_ids=list(range(8)))
    B, SEQ = 32, 256
    y_p = np.zeros((B, SEQ, D), np.float32)
    y_s = np.zeros((2, 1024, D), np.float32)
    nk = np.zeros((B, DEPTH, SEQ, 4, 64), np.float32)
    nv = np.zeros((B, DEPTH, SEQ, 4, 64), np.float32)
    for i in range(8):
        r = res.results[i]
        y = np.asarray(r["yT"]).T
        k_ = np.asarray(r["nk"])
        v_ = np.asarray(r["nv"])
        is_s, seqs, bseq = _core_units(i)
        if is_s:
            y_s[seqs[0]] = y[0:1024]
        else:
            for j, s in enumerate(seqs):
                y_p[s] = y[j * 256:(j + 1) * 256]
                nk[s] = k_[:, j * 256:(j + 1) * 256].reshape(DEPTH, SEQ, 4, 64)
                nv[s] = v_[:, j * 256:(j + 1) * 256].reshape(DEPTH, SEQ, 4, 64)
        y_p[bseq] = y[1024:1280]
        nk[bseq] = k_[:, 1024:1280].reshape(DEPTH, SEQ, 4, 64)
        nv[bseq] = v_[:, 1024:1280].reshape(DEPTH, SEQ, 4, 64)
    return (y_p, y_s, nk, nv)
```

```python
import math
from contextlib import ExitStack

import numpy as np
import ml_dtypes

import concourse.bass as bass
import concourse.mybir as mybir
from concourse.bass_utils import run_bass_kernel_spmd

F32 = mybir.dt.float32
BF16 = mybir.dt.bfloat16
AF = mybir.ActivationFunctionType
ALU = mybir.AluOpType
AX = mybir.AxisListType

D = 1024
DEPTH = 4
NT = 1280
NTT = 10
BLK = [(0, 512), (512, 512), (1024, 256)]
EPS = 1e-6
NSLOT = 4
BIG = 30000.0
QSCALE = 32 ** -0.5
DUMMY_N = 0

O_BADA, O_GATT, O_GMLP, O_GQK, O_GSG, O_BSG, O_GHEAD, O_LAM = 0, 48, 56, 64, 128, 384, 640, 641
O_WDW, O_BDW, O_GCV, O_BCV, O_BSP = 769, 831, 833, 835, 837
NF = 1096


class Buf:
    __slots__ = ("name", "lw", "rdc", "rdd")

    def __init__(self, name):
        self.name = name
        self.lw = None
        self.rdc = {}
        self.rdd = []


class DGrp:
    __slots__ = ("name", "sem", "count")

    def __init__(self, name):
        self.name = name
        self.sem = None
        self.count = 0


class Op:
    __slots__ = ("eng", "fn", "waits", "kind", "val", "dgrp")


class Sched:
    ENG = ["pe", "act", "dve", "pool", "sp"]

    def __init__(self, nc, stack):
        self.nc = nc
        self.stack = stack
        self.ops = []
        self.sem = {e: stack.enter_context(nc.semaphore("sem_" + e)) for e in self.ENG}
        self.cnt = {e: 0 for e in self.ENG}
        self.waited = {e: {} for e in self.ENG}
        self.semobj = {("c", e): self.sem[e] for e in self.ENG}
        self.dgrps = []

    def dgrp(self, name):
        g = DGrp(name)
        g.sem = self.stack.enter_context(self.nc.semaphore("dsem_" + name))
        self.semobj[("d", name)] = g.sem
        self.dgrps.append(g)
        return g

    def add(self, eng, fn, reads=(), writes=(), dgrp=None):
        op = Op()
        op.eng = eng
        op.fn = fn
        op.dgrp = dgrp
        op.kind = "d" if dgrp is not None else "c"
        need = {}

        def dep(d):
            if d is None:
                return
            if d.kind == "c":
                if d.eng == eng and eng == "pe" and op.kind == "c":
                    return
                key = ("c", d.eng)
                val = d.val
            else:
                key = ("d", d.dgrp.name)
                val = d.dgrp.count
            if need.get(key, 0) < val:
                need[key] = val

        for b in reads:
            dep(b.lw)
        for b in writes:
            dep(b.lw)
            for r in b.rdc.values():
                dep(r)
            for r in b.rdd:
                dep(r)
        op.waits = []
        w = self.waited[eng]
        for key, val in need.items():
            if w.get(key, 0) >= val:
                continue
            w[key] = val
            op.waits.append((self.semobj[key], val))
        if op.kind == "d":
            dgrp.count += 16
            op.val = dgrp.count
        else:
            self.cnt[eng] += 1
            op.val = self.cnt[eng]
        for b in reads:
            if op.kind == "c":
                b.rdc[eng] = op
            else:
                b.rdd.append(op)
        for b in writes:
            b.lw = op
            b.rdc = {}
            b.rdd = []
        self.ops.append(op)
        return op

    def emit(self):
        nc = self.nc
        with nc.Block() as block:
            decos = {"pe": block.tensor, "act": block.scalar, "dve": block.vector,
                     "pool": block.gpsimd, "sp": block.sync}
            for eng in self.ENG:
                def body(e, eng=eng):
                    for op in self.ops:
                        if op.eng != eng:
                            continue
                        for (sem, v) in op.waits:
                            e.wait_ge(sem, v)
                        ins = op.fn(e)
                        if op.kind == "c":
                            ins.then_inc(self.sem[eng], 1)
                        else:
                            ins.then_inc(op.dgrp.sem, 16)
                    if eng == "sp":
                        for g in self.dgrps:
                            if g.count > 0:
                                e.wait_ge(g.sem, g.count)
                decos[eng](body)


class Ring:
    def __init__(self, items):
        self.items = items
        self.i = 0

    def next(self):
        it = self.items[self.i % len(self.items)]
        self.i += 1
        return it


def bcast(ap, axis, n):
    shp = list(ap.shape)
    shp.insert(axis, n)
    return ap.unsqueeze(axis).broadcast_to(shp)


class _Stop(Exception):
    pass


def build(n_layers=DEPTH, debug=False, stop=None):
    nc = bass.Bass("TRN2", target_bir_lowering=False)
    L = n_layers

    def din(name, shape, dt=F32):
        return nc.dram_tensor(name, list(shape), dt, kind="ExternalInput").ap()

    def dout(name, shape, dt=F32):
        return nc.dram_tensor(name, list(shape), dt, kind="ExternalOutput").ap()

    xT_d = din("xT", [D, NT])
    cond_d = din("condL", [128, 8, 2])
    ck_d = din("ck", [DEPTH, 256, 256])
    cv_d = din("cv", [DEPTH, 256, 256])
    cs_d = din("cs", [128, NTT, 32])
    ind_d = din("indt", [128, 12, 2, 8], BF16)
    flag_d = din("flag", [128, 1])
    csc_d = din("csc", [128, 256], BF16)
    dfa_d = din("dfa", [4, 128, 4096], BF16)
    dfb_d = din("dfb", [128, 2, 2, 256], BF16)
    ident_d = din("ident", [128, 128], BF16)
    pf_d = din("pf", [DEPTH, 128, NF])
    pa_d = din("pa", [DEPTH, 128, 64])
    wsT_d = din("wsT", [DEPTH, 128, 512])
    w_ada = din("w_ada", [DEPTH, D, 6 * D])
    w_in = din("w_in", [DEPTH, D, 2048])
    w_out = din("w_out", [DEPTH, D, D])
    w_ff1 = din("w_ff1", [DEPTH, D, 4096])
    w_ff2 = din("w_ff2", [DEPTH, 4096, D])

    yT_d = dout("yT", [D, NT])
    nk_d = dout("nk", [DEPTH, NT, 256])
    nv_d = dout("nv", [DEPTH, NT, 256])
    dbg_d = {}

    with ExitStack() as st:
        S = Sched(nc, st)

        def sb(name, shape, dt):
            return st.enter_context(nc.sbuf_tensor(name, list(shape), dt))

        xT = sb("xT_s", [128, 8, NT], F32)
        hT = sb("hT_s", [128, 8, NT], BF16)
        bigT = sb("bigT_s", [128, 8, NT], BF16)
        ring = [sb("ring%d" % i, [128, 4096], BF16) for i in range(NSLOT)]
        QT = sb("QT_s", [128, 4, NT], BF16)
        KT = sb("KT_s", [128, 4, NT + 256], BF16)
        Vaug = sb("Vaug_s", [128, 12, 4, 128], BF16)
        qkaug = [sb("qkaug%d" % i, [128, 16, 64], BF16) for i in range(2)]
        ctxa = sb("ctxa_s", [128, 8, 64], BF16)
        uG = sb("uG_s", [128, 2, NT], BF16)
        hcpad = sb("hcpad_s", [128, 2, 5, 286], BF16)
        cacc = sb("cacc_s", [128, 2, 512], F32)
        zdT = sb("zdT_s", [128, 2, NT], BF16)
        vnr = [sb("vnr%d" % i, [128, 256], BF16) for i in range(2)]
        Er = [sb("Er%d" % i, [128, 512], BF16) for i in range(3)]
        tF = [sb("tF%d" % i, [128, 512], F32) for i in range(4)]
        tB = [sb("tB%d" % i, [128, 512], BF16) for i in range(2)]
        qkn = [sb("qkn%d" % i, [128, 512], F32) for i in range(2)]
        rstd = [sb("rstd%d" % i, [128, 512], F32) for i in range(1)]
        stt = [sb("stt%d" % i, [128, 64], F32) for i in range(2)]
        PF = sb("PF_s", [128, NF], F32)
        wsT = sb("wsT_s", [128, 4, 128], BF16)
        modS2 = [sb("modS%d" % i, [128, 48, 2], F32) for i in range(2)]
        modD2 = [sb("modD%d" % i, [128, 2, 8, 2], F32) for i in range(2)]
        PA2 = [sb("PA%d" % i, [128, 64], F32) for i in range(2)]
        lamt = sb("lamt_s", [128, 40], F32)
        ident = sb("ident_s", [128, 128], BF16)
        onesb = sb("onesb_s", [128, 128], BF16)
        onesf = sb("onesf_s", [128, 128], F32)
        csc = sb("csc_s", [128, 256], BF16)
        dfb = sb("dfb_s", [128, 2, 2, 256], BF16)
        cs = sb("cs_s", [128, NTT, 32], F32)
        indt = sb("indt_s", [128, 12, 2, 8], BF16)
        flag = sb("flag_s", [128, 1], F32)
        condL = sb("condL_s", [128, 8, 2], F32)
        sT = sb("sT_s", [128, 8, 2], BF16)
        Y1 = QT[:, :, :].rearrange("p a b -> p (a b)").rearrange("p (t c n) -> p t c n", t=NTT, c=2)
        qtf = QT[:, :, :].rearrange("p a b -> p (a b)").bitcast(F32)
        vst = [qtf[:, r_ * 256:(r_ + 1) * 256] for r_ in range(2)]
        gvt = [qtf[:, 512 + r_ * 256:512 + (r_ + 1) * 256] for r_ in range(2)]
        ropet = [[qtf[:, 1024 + (2 * r_ + i_) * 256:1024 + (2 * r_ + i_ + 1) * 256] for i_ in range(2)]
                 for r_ in range(2)]
        bank = [st.enter_context(nc.psum_tensor("bank%d" % i, [128, 512], F32)) for i in range(8)]

        def B2(name, n, m):
            return [[Buf("%s_%d_%d" % (name, i, j)) for j in range(m)] for i in range(n)]

        def B1(name, n):
            return [Buf("%s_%d" % (name, i)) for i in range(n)]

        xTb = B2("xT", 8, 3)
        hTb = B2("hT", 8, 3)
        bigb = B2("big", 8, 3)
        slotb = B1("slot", NSLOT)
        bankb = B1("bank", 8)
        QTt = B1("QT", NTT)
        QmTb = B1("QmT", NTT)
        KTt = B1("KT", 12)
        Vt = B1("V", 12)
        qkaugb = B1("qkaug", 2)
        ctxab = Buf("ctxa")
        uGb = B2("uG", 2, 3)
        hcpb = B2("hcp", 2, 3)
        caccb = Buf("cacc")
        zdTb = B2("zdT", 2, 3)
        vnrb, Erb, tFb, tBb = B1("vnr", 2), B1("Er", 3), B1("tF", 4), B1("tB", 2)
        qknb, vstb, gvtb, ropetb, rstdb, sttb = (B1("qkn", 2), B1("vst", 2), B1("gvt", 2),
                                                 B2("ropet", 2, 2), B1("rstd", 1), B1("stt", 2))
        PFb, wsTb, lamb = Buf("PF"), Buf("wsT"), Buf("lam")
        modSb2, modDb2, PAb2 = B1("modS", 2), B1("modD", 2), B1("PA", 2)
        constb = Buf("const")
        cvtb, dummyb, cvt2b = Buf("cvt"), Buf("dummy"), Buf("cvt2")
        dgb = B1("dg", 8)
        sTb = Buf("sT")

        slot_g = [S.dgrp("slot%d" % i) for i in range(NSLOT)]
        g_const = S.dgrp("const")
        g_x = S.dgrp("xload")
        g_pf = S.dgrp("pf")
        g_pa = [S.dgrp("pa%d" % i) for i in range(2)]
        g_ws = S.dgrp("ws")
        g_y = S.dgrp("yout")
        g_kst = [S.dgrp("kst%d" % i) for i in range(2)]
        g_vst = [S.dgrp("vst%d" % i) for i in range(2)]
        g_dbg = S.dgrp("dbg")

        tFr = Ring(list(zip(tF, tFb)))
        tBr = Ring(list(zip(tB, tBb)))
        Err = Ring(list(zip(Er, Erb)))
        rstdr = Ring(list(zip(rstd, rstdb)))

        ring_state = {"i": 0}

        def load_slot(src_ap, cast=True):
            i = ring_state["i"] % NSLOT
            ring_state["i"] += 1
            t, b, g = ring[i], slotb[i], slot_g[i]
            shp = src_ap.shape
            dst = t[:, :].rearrange("p (a b) -> p a b", a=shp[1])
            S.add("pool", lambda e, dst=dst, src=src_ap: e.dma_start(out=dst, in_=src),
                  writes=[b], dgrp=g)
            return t, b

        def wslot_cols(w_l, c0):
            return w_l[:, c0:c0 + 512].rearrange("(k p) n -> p k n", p=128)

        def wslot_rows(w_l, r0):
            return w_l[r0:r0 + 512, :].rearrange("(f p) n -> p f n", p=128)

        def mm_group(out_ap, pairs, reads, writes):
            n = len(pairs)

            def fn(e, out_ap=out_ap, pairs=pairs, n=n):
                ins = None
                for i, (l, r) in enumerate(pairs):
                    ins = e.matmul(out_ap, l, r, start=(i == 0), stop=(i == n - 1))
                return ins
            return S.add("pe", fn, reads=reads, writes=writes)

        def act(out, in_, func, reads, writes, **kw):
            return S.add("act", lambda e: e.activation(out=out, in_=in_, func=func, **kw),
                         reads=reads, writes=writes)

        def dve_tt(out, in0, in1, op, reads, writes, eng="dve"):
            return S.add(eng, lambda e: e.tensor_tensor(out=out, in0=in0, in1=in1, op=op),
                         reads=reads, writes=writes)

        def dve_ts(out, in0, s1, s2, op0, op1, reads, writes, eng="dve"):
            if op1 is None:
                return S.add(eng, lambda e: e.tensor_scalar(out=out, in0=in0, scalar1=s1, scalar2=None,
                                                            op0=op0), reads=reads, writes=writes)
            return S.add(eng, lambda e: e.tensor_scalar(out=out, in0=in0, scalar1=s1, scalar2=s2,
                                                        op0=op0, op1=op1), reads=reads, writes=writes)

        def dve_stt(out, in0, scalar, in1, op0, op1, reads, writes):
            return S.add("dve", lambda e: e.scalar_tensor_tensor(out=out, in0=in0, scalar=scalar, in1=in1,
                                                                 op0=op0, op1=op1),
                         reads=reads, writes=writes)

        def copy(eng, out, in_, reads, writes):
            if eng == "act":
                return S.add("act", lambda e: e.activation(out=out, in_=in_, func=AF.Copy),
                             reads=reads, writes=writes)
            return S.add(eng, lambda e: e.tensor_copy(out=out, in_=in_), reads=reads, writes=writes)

        def rstd_op(out, in_, scale, reads_in, tmp, tmpb, outb):
            act(tmp, in_, AF.Ln, reads_in, [tmpb], scale=scale, bias=eps_ap(in_))
            act(out, tmp, AF.Exp, [tmpb], [outb], scale=-0.5)

        epsb = sb("eps_s", [128, 1], F32)
        dummy = sb("dummy_s", [128, 2], F32)
        cmask = sb("cmask_s", [128, 2], F32)
        dg8 = sb("dg8_s", [128, 8 * 128], BF16)

        def eps_ap(in_):
            p0 = in_.start_partition()
            n = in_.partition_size()
            return epsb[p0:p0 + n, 0:1]

        def dbg_dump(name, src_ap, shape, reads, dt=F32):
            if not debug:
                return
            d = dout("dbg_" + name, shape, dt)
            dbg_d[name] = d
            S.add("sp", lambda e: e.dma_start(out=d, in_=src_ap), reads=reads, dgrp=g_dbg)

        def cload(dst, src):
            S.add("sp", lambda e: e.dma_start(out=dst, in_=src), writes=[constb], dgrp=g_const)

        cload(ident[:, :], ident_d[:, :])
        cload(csc[:, :], csc_d[:, :])
        cload(dfb[:, :, :, :], dfb_d[:, :, :, :])
        cload(cs[:, :, :], cs_d[:, :, :])
        cload(indt[:, :, :, :], ind_d[:, :, :, :])
        cload(flag[:, :], flag_d[:, :])
        cload(condL[:, :, :], cond_d[:, :, :])
        for k in range(8):
            S.add("sp", lambda e, k=k: e.dma_start(out=xT[:, k, :], in_=xT_d[k * 128:(k + 1) * 128, :]),
                  writes=xTb[k], dgrp=g_x)
        S.add("pool", lambda e: e.memset(onesb[:, :], 1.0), writes=[constb])
        S.add("pool", lambda e: e.memset(onesf[:, :], 1.0), writes=[constb])
        S.add("pool", lambda e: e.memset(epsb[:, :], EPS), writes=[constb])
        S.add("pool", lambda e: e.memset(cmask[:, :], 0.0), writes=[constb])
        S.add("pool", lambda e: e.memset(cmask[0:64, 0:1], 1.0), writes=[constb])
        S.add("pool", lambda e: e.memset(cmask[64:128, 1:2], 1.0), writes=[constb])
        S.add("pool", lambda e: e.memset(Vaug[:, :, :, :], 1.0), writes=Vt)
        for r in range(2):
            S.add("pool", lambda e, r=r: e.memset(qkaug[r][:, :, :], 0.0), writes=[qkaugb[r]])
        S.add("pool", lambda e: e.memset(ctxa[:, :, :], 0.0), writes=[ctxab])
        S.add("pool", lambda e: e.memset(hcpad[:, :, :, :], 0.0), writes=hcpb[0] + hcpb[1])
        act(sT[:, :, :], condL[:, :, :], AF.Silu, [constb], [sTb])

        def chk(name):
            if stop == name:
                raise _Stop()

        def mod_gen(l, first=0, last=12):
            ms = l % 2
            modS, modD, modSb, modDb, PA, PAb = modS2[ms], modD2[ms], modSb2[ms], modDb2[ms], PA2[ms], PAb2[ms]
            if first == 0:
                S.add("sp", lambda e: e.dma_start(out=PA[:, :], in_=pa_d[l, :, :]), writes=[PAb], dgrp=g_pa[ms])
            pm, pmb = bank[7], bankb[7]
            slots = {}
            for s_ in range(first, min(first + 3, last)):
                slots[s_] = load_slot(wslot_cols(w_ada[l], s_ * 512))
                yield
            for s_ in range(first, last):
                t, b = slots[s_]
                tv = t[:, :].rearrange("p (k n) -> p k n", k=8)
                for j in range(4):
                    mm_group(pm[:, 2 * j:2 * j + 2],
                             [(tv[:, k, j * 128:(j + 1) * 128], sT[:, k, :]) for k in range(8)],
                             reads=[b, sTb], writes=[pmb])
                dve_tt(modS[:, 4 * s_:4 * s_ + 4, :], pm[:, 0:8].rearrange("p (n c) -> p n c", c=2),
                       bcast(PA[:, 4 * s_:4 * s_ + 4], 2, 2), ALU.add, [pmb, PAb], [modSb])
                if s_ + 3 < last:
                    slots[s_ + 3] = load_slot(wslot_cols(w_ada[l], (s_ + 3) * 512))
                if s_ == 3:
                    dve_stt(modD[:, 0, :, :], modS[:, 8:16, :], 1.0, bcast(PA[:, 48:56], 2, 2), ALU.add, ALU.mult,
                            [modSb, PAb], [modDb])
                if s_ == 9:
                    dve_stt(modD[:, 1, :, :], modS[:, 32:40, :], 1.0, bcast(PA[:, 56:64], 2, 2), ALU.add, ALU.mult,
                            [modSb, PAb], [modDb])
                yield

        def layer(l):
            lam_init = 0.8 - 0.6 * math.exp(-0.3 * l)
            S.add("sp", lambda e: e.dma_start(out=PF[:, :], in_=pf_d[l, :, :]), writes=[PFb], dgrp=g_pf)
            S.add("pool", lambda e: e.dma_start(out=wsT[:, :, :],
                                                in_=wsT_d[l, :, :].rearrange("p (g q) -> p g q", g=4)),
                  writes=[wsTb], dgrp=g_ws)
            ckst = qkn[0][:, :].rearrange("p (t n) -> p t n", t=2)
            cvst = qkn[1][:, :].rearrange("p (t n) -> p t n", t=2)
            ckstb, cvstb = qknb[0], qknb[1]
            S.add("sp", lambda e: e.dma_start(out=ckst, in_=ck_d[l, :, :].rearrange("(t p) n -> p t n", p=128)),
                  writes=[ckstb], dgrp=g_kst[0])
            S.add("sp", lambda e: e.dma_start(out=cvst, in_=cv_d[l, :, :].rearrange("(t p) n -> p t n", p=128)),
                  writes=[cvstb], dgrp=g_kst[1])

            lq1 = PF[:, O_LAM:O_LAM + 32]
            lk1 = PF[:, O_LAM + 32:O_LAM + 64]
            lq2 = PF[:, O_LAM + 64:O_LAM + 96]
            lk2 = PF[:, O_LAM + 96:O_LAM + 128]
            dve_tt(lamt[:, 0:32], lq1, lk1, ALU.mult, [PFb], [lamb])
            S.add("dve", lambda e: e.tensor_reduce(out=lamt[:, 32:33], in_=lamt[:, 0:32], axis=AX.X, op=ALU.add),
                  reads=[lamb], writes=[lamb])
            dve_tt(lamt[:, 0:32], lq2, lk2, ALU.mult, [PFb, lamb], [lamb])
            S.add("dve", lambda e: e.tensor_reduce(out=lamt[:, 33:34], in_=lamt[:, 0:32], axis=AX.X, op=ALU.add),
                  reads=[lamb], writes=[lamb])
            act(lamt[:, 34:36], lamt[:, 32:34], AF.Exp, [lamb], [lamb])
            dve_tt(lamt[:, 36:37], lamt[:, 35:36], lamt[:, 34:35], ALU.subtract, [lamb], [lamb])
            dve_ts(lamt[:, 36:37], lamt[:, 36:37], -lam_init, None, ALU.add, None, [lamb], [lamb])
            dve_ts(lamt[:, 37:38], PF[:, O_GHEAD:O_GHEAD + 1], 1.0 - lam_init, None, ALU.mult, None,
                   [PFb, lamb], [lamb])
            neglam = lamt[:, 36:37]
            ghs = lamt[:, 37:38]

            chk('lam')
            ms = l % 2
            modS, modD, modSb, modDb = modS2[ms], modD2[ms], modSb2[ms], modDb2[ms]
            if l == 0:
                for _ in mod_gen(0, 0, 4):
                    pass

            chk('mod')

            def mod_ap(split, k, c):
                return modS[:, split * 8 + k, c:c + 1]

            def norm_blk(which, bi, lyr):
                mS, mD, mSb, mDb = modS2[lyr % 2], modD2[lyr % 2], modSb2[lyr % 2], modDb2[lyr % 2]
                gidx = 0 if which == 1 else 1
                shs = 0 if which == 1 else 3
                t0, n = BLK[bi]
                c = 0 if bi < 2 else 1
                pb, pbb = bank[bi], bankb[bi]
                for k in range(8):
                    sq, sqb = tBr.next()
                    act(sq[:, 0:n], xT[:, k, t0:t0 + n], AF.Square, [xTb[k][bi]], [sqb])
                    S.add("pe", (lambda e, pb=pb, sq=sq, n=n, k=k:
                                 e.matmul(pb[:, 0:n], onesb[:, :], sq[:, 0:n], start=(k == 0), stop=(k == 7))),
                          reads=[sqb, constb], writes=[pbb])
                rs, rsb = rstdr.next()
                tmp, tmpb = tFr.next()
                rstd_op(rs[:, 0:n], pb[:, 0:n], 1.0 / D, [pbb, constb], tmp[:, 0:n], tmpb, rsb)
                for k in range(8):
                    tmp, tmpb = tFr.next()
                    dve_stt(tmp[:, 0:n], xT[:, k, t0:t0 + n], mD[:, gidx, k, c:c + 1], rs[:, 0:n],
                            ALU.mult, ALU.mult, [xTb[k][bi], mDb, rsb], [tmpb])
                    act(hT[:, k, t0:t0 + n], tmp[:, 0:n], AF.Identity, [tmpb, mSb], [hTb[k][bi]],
                        bias=mS[:, shs * 8 + k, c:c + 1])

            if l == 0:
                for bi_ in range(3):
                    norm_blk(1, bi_, 0)
            if debug and l == 0:
                dbg_dump("hT", hT[:, :, :], [128, 8, NT], [b for r in hTb for b in r], BF16)
                dbg_dump("modS", modS[:, :, :], [128, 48, 2], [modSb])
            chk('norm1')

            wi = [load_slot(wslot_cols(w_in[l], c * 512)) for c in range(4)]
            wiv = [t[:, :].rearrange("p (k n) -> p k n", k=8) for (t, b) in wi]
            wib = [b for (t, b) in wi]
            hall = lambda bi: [hTb[k][bi] for k in range(8)]
            pbr = Ring(list(range(0, 7)))

            for bi, (t0, n) in enumerate(BLK):
                segs = (2 * bi, 2 * bi + 2) if bi < 2 else (4, 5)
                nseg = segs[1] - segs[0]
                for j in range(2):
                    p = pbr.next()
                    mm_group(bank[p][:, 0:n],
                             [(wiv[1][:, k, 256 + j * 128:256 + (j + 1) * 128], hT[:, k, t0:t0 + n]) for k in range(8)],
                             hall(bi) + [wib[1]], [bankb[p]])
                    act(uG[:, j, t0:t0 + n], bank[p][:, 0:n], AF.Gelu_apprx_tanh, [bankb[p]], [uGb[j][bi]])
                    chk('feat_u')
                    pa = pbr.next()
                    mm_group(bank[pa][:, 0:n],
                             [(wiv[2][:, k, 256 + j * 128:256 + (j + 1) * 128], hT[:, k, t0:t0 + n]) for k in range(8)],
                             hall(bi) + [wib[2]], [bankb[pa]])
                    pg = pbr.next()
                    mm_group(bank[pg][:, 0:n],
                             [(wiv[3][:, k, j * 128:(j + 1) * 128], hT[:, k, t0:t0 + n]) for k in range(8)],
                             hall(bi) + [wib[3]], [bankb[pg]])
                    chk('feat_mm')
                    sg, sgb = tFr.next()
                    act(sg[:, 0:n], bank[pg][:, 0:n], AF.Sigmoid, [bankb[pg]], [sgb])
                    dve_tt(hcpad[:, j, segs[0]:segs[1], 15:271],
                           bank[pa][:, 0:n].rearrange("p (s t) -> p s t", s=nseg),
                           sg[:, 0:n].rearrange("p (s t) -> p s t", s=nseg), ALU.mult,
                           [bankb[pa], sgb], [hcpb[j][bi]])
                    chk('feat_conv')
                    pz = pbr.next()
                    mm_group(bank[pz][:, 0:n],
                             [(wiv[3][:, k, 256 + j * 128:256 + (j + 1) * 128], hT[:, k, t0:t0 + n]) for k in range(8)],
                             hall(bi) + [wib[3]], [bankb[pz]])
                    copy("act", zdT[:, j, t0:t0 + n], bank[pz][:, 0:n], [bankb[pz]], [zdTb[j][bi]])

            chk('feat')
            for ct in range(2):
                S.add("pool", lambda e, ct=ct: e.tensor_copy(
                    out=ctxa[:, :, 0:32], in_=ckst[:, ct, :].rearrange("p (g d) -> p g d", g=8)),
                    reads=[ckstb], writes=[ctxab])
                S.add("pool", lambda e, ct=ct: e.tensor_copy(
                    out=ctxa[:, :, 32:40], in_=bcast(indt[:, 10 + ct, 1, :], 1, 8)),
                    reads=[constb], writes=[ctxab])
                p = pbr.next()
                pv = bank[p][:, :].bitcast(BF16)

                def tr_ctx(e, ct=ct, pv=pv):
                    ins = None
                    for h in range(4):
                        ins = e.transpose(out=pv[:, h * 128:(h + 1) * 128],
                                          in_=ctxa[:, 2 * h:2 * h + 2, :].rearrange("p a b -> p (a b)"),
                                          identity=ident[:, :])
                    return ins
                S.add("pe", tr_ctx, reads=[ctxab, constb], writes=[bankb[p]])
                copy("dve", KT[:, :, NT + ct * 128:NT + (ct + 1) * 128],
                     pv[:, 0:512].rearrange("p (h t) -> p h t", h=4), [bankb[p]], [KTt[10 + ct]])
                cvv = cvst[:, ct, :].rearrange("p (a b d) -> p a b d", a=2, b=2)
                va = Vaug[:, 10 + ct, :, :].rearrange("p (a b) d -> p a b d", a=2)
                S.add("pool", lambda e, cvv=cvv, va=va: e.tensor_copy(out=va[:, :, 0, 0:64], in_=cvv[:, :, 0, :]),
                      reads=[cvstb], writes=[Vt[10 + ct]])
                S.add("pool", lambda e, cvv=cvv, va=va: e.tensor_copy(out=va[:, :, 1, 64:128], in_=cvv[:, :, 1, :]),
                      reads=[cvstb], writes=[Vt[10 + ct]])

            chk('ctx')
            tmpall = vstb + gvtb + ropetb[0] + ropetb[1]
            S.add("dve", lambda e: e.memset(dummy[:, 1:2], 0.0), writes=QTt + tmpall + [dummyb])
            tst = {}

            def stageA(tt):
                st_ = tst.setdefault(tt, {})
                bi = min(tt // 4, 2)
                r = tt % 2
                c0 = tt * 128
                pA = 2 * (tt % 3)
                pB = 2 * (tt % 3) + 1
                mm_group(bank[pA][:, :], [(hT[:, k, c0:c0 + 128], wiv[0][:, k, :]) for k in range(8)],
                         hall(bi) + [wib[0]], [bankb[pA]])
                mm_group(bank[pB][:, 0:256], [(hT[:, k, c0:c0 + 128], wiv[1][:, k, 0:256]) for k in range(8)],
                         hall(bi) + [wib[1]], [bankb[pB]])
                mm_group(bank[pB][:, 256:512], [(hT[:, k, c0:c0 + 128], wiv[2][:, k, 0:256]) for k in range(8)],
                         hall(bi) + [wib[2]], [bankb[pB]])

                st_.update(bi=bi, r=r, c0=c0, pA=pA, pB=pB)

            def genB1(tt):
                st_ = tst[tt]
                bi, r, c0, pA, pB = st_["bi"], st_["r"], st_["c0"], st_["pA"], st_["pB"]
                qn, qnb = qkn[r], qknb[r]
                act(qn[:, :], bank[pA][:, :], AF.Square, [bankb[pA]], [qnb])
                yield
                sv, svb = stt[r], sttb[r]
                S.add("dve", lambda e, qn=qn, sv=sv: e.tensor_reduce(
                    out=sv[:, 0:16], in_=qn[:, :].rearrange("p (g d) -> p g d", g=16), axis=AX.X, op=ALU.add),
                    reads=[qnb], writes=[svb])
                yield
                rstd_op(sv[:, 32:48], sv[:, 0:16], 1.0 / 32, [svb, constb], sv[:, 16:32], svb, svb)
                yield
                dve_tt(qn[:, :].rearrange("p (g d) -> p g d", g=16),
                       bank[pA][:, :].rearrange("p (g d) -> p g d", g=16),
                       bcast(sv[:, 32:48], 2, 32), ALU.mult, [bankb[pA], svb, qnb], [qnb])
                yield
                gqk = PF[:, O_GQK:O_GQK + 64].rearrange("p (a d) -> p a d", a=2)
                dve_tt(qn[:, :].rearrange("p (a g d) -> p a g d", a=2, g=8),
                       qn[:, :].rearrange("p (a g d) -> p a g d", a=2, g=8),
                       bcast(gqk, 2, 8), ALU.mult, [qnb, PFb], [qnb])
                yield
                S.add("sp", lambda e, qn=qn, c0=c0: e.dma_start(out=nk_d[l, c0:c0 + 128, :], in_=qn[:, 256:512]),
                      reads=[qnb], dgrp=g_kst[r])
                yield
                qv = qn[:, :].rearrange("p (g x h f) -> p g x h f", g=16, x=2, h=2)
                av, bv = qv[:, :, :, 0, :], qv[:, :, :, 1, :]
                cosv = bcast(cs[:, tt, 0:16].rearrange("p (x f) -> p x f", x=2), 1, 16)
                sinv = bcast(cs[:, tt, 16:32].rearrange("p (x f) -> p x f", x=2), 1, 16)
                qa, qab = qkaug[r], qkaugb[r]
                ov = qa[:, :, 0:32].rearrange("p g (x h f) -> p g x h f", x=2, h=2)
                oa, ob = ov[:, :, :, 0, :], ov[:, :, :, 1, :]
                r1 = ropet[r][0].rearrange("p (g x f) -> p g x f", g=16, x=2)
                r2 = ropet[r][1].rearrange("p (g x f) -> p g x f", g=16, x=2)
                ropetb_ = ropetb[r]
                P = "pool"
                dve_tt(r1, av, cosv, ALU.mult, [qnb, constb], [ropetb_[0]], eng=P)
                yield
                dve_tt(r2, bv, sinv, ALU.mult, [qnb, constb], [ropetb_[1]], eng=P)
                yield
                dve_tt(oa, r1, r2, ALU.subtract, [ropetb_[0], ropetb_[1]], [qab], eng=P)
                yield
                dve_tt(r1, bv, cosv, ALU.mult, [qnb, constb], [ropetb_[0]], eng=P)
                yield
                dve_tt(r2, av, sinv, ALU.mult, [qnb, constb], [ropetb_[1]], eng=P)
                yield
                dve_tt(ob, r1, r2, ALU.add, [ropetb_[0], ropetb_[1]], [qab], eng=P)
                yield
                S.add("pool", lambda e, qa=qa, tt=tt: e.tensor_copy(
                    out=qa[:, :, 32:40].rearrange("p (a g) c -> p a g c", a=2),
                    in_=bcast(indt[:, tt, :, :], 2, 8)), reads=[constb], writes=[qab])
                yield
                st_.update(qa=qa, qab=qab)

            def genB2(tt):
                st_ = tst[tt]
                bi, r, c0, pA, pB = st_["bi"], st_["r"], st_["c0"], st_["pA"], st_["pB"]
                sv, svb = stt[r], sttb[r]
                vs, vsb = vst[r], vstb[r]
                copy("act", vs, bank[pB][:, 0:256], [bankb[pB]], [vsb])
                yield
                S.add("sp", lambda e, vs=vs, c0=c0: e.dma_start(out=nv_d[l, c0:c0 + 128, :], in_=vs),
                      reads=[vsb], dgrp=g_vst[r])
                yield
                vv = vs.rearrange("p (a b d) -> p a b d", a=2, b=2)
                va = Vaug[:, tt, :, :].rearrange("p (a b) d -> p a b d", a=2)
                S.add("pool", lambda e, vv=vv, va=va: e.tensor_copy(out=va[:, :, 0, 0:64], in_=vv[:, :, 0, :]),
                      reads=[vsb], writes=[Vt[tt]])
                yield
                S.add("pool", lambda e, vv=vv, va=va: e.tensor_copy(out=va[:, :, 1, 64:128], in_=vv[:, :, 1, :]),
                      reads=[vsb], writes=[Vt[tt]])
                yield
                gv, gvb = gvt[r], gvtb[r]
                S.add("act", lambda e, gv=gv, pB=pB, sv=sv: e.activation(
                    out=gv, in_=bank[pB][:, 256:512], func=AF.Gelu_apprx_tanh, accum_out=sv[:, 48:49]),
                    reads=[bankb[pB]], writes=[gvb, svb])
                yield
                S.add("act", lambda e, gv=gv, pB=pB, sv=sv: e.activation(
                    out=bank[pB][:, 256:512], in_=gv, func=AF.Square, accum_out=sv[:, 49:50]),
                    reads=[gvb], writes=[bankb[pB], svb])
                yield
                dve_ts(sv[:, 50:51], sv[:, 48:49], 1.0 / 256, None, ALU.mult, None, [svb], [svb])
                yield
                dve_tt(sv[:, 51:52], sv[:, 50:51], sv[:, 50:51], ALU.mult, [svb], [svb])
                yield
                dve_stt(sv[:, 52:53], sv[:, 49:50], 1.0 / 256, sv[:, 51:52], ALU.mult, ALU.subtract,
                        [svb], [svb])
                yield
                rstd_op(sv[:, 54:55], sv[:, 52:53], 1.0, [svb, constb], sv[:, 53:54], svb, svb)
                yield
                dve_ts(gv, gv, sv[:, 50:51], sv[:, 54:55], ALU.subtract, ALU.mult,
                       [gvb, svb], [gvb])
                yield
                dve_tt(gv, gv, PF[:, O_GSG:O_GSG + 256], ALU.mult, [gvb, PFb], [gvb])
                yield
                vn, vnb = vnr[r], vnrb[r]
                dve_tt(vn[:, :], gv, PF[:, O_BSG:O_BSG + 256], ALU.add, [gvb, PFb], [vnb])
                yield

                st_.update(vn=vn, vnb=vnb)

            def stageC(tt):
                st_ = tst[tt]
                bi, r, c0 = st_["bi"], st_["r"], st_["c0"]
                qa, qab, vn, vnb = st_["qa"], st_["qab"], st_["vn"], st_["vnb"]
                pT = 6
                pv = bank[pT][:, :].bitcast(BF16)

                def tr_qk(e, qa=qa, pv=pv):
                    ins = None
                    for j in range(8):
                        ins = e.transpose(out=pv[:, j * 128:(j + 1) * 128],
                                          in_=qa[:, 2 * j:2 * j + 2, :].rearrange("p a b -> p (a b)"),
                                          identity=ident[:, :])
                    return ins
                S.add("pe", tr_qk, reads=[qab, constb], writes=[bankb[pT]])
                for c in range(2):
                    dve_ts(hT[:, 4 * c:4 * c + 4, c0:c0 + 128], pv[:, 0:512].rearrange("p (h t) -> p h t", h=4),
                           cmask[:, c:c + 1], None, ALU.mult, None, [bankb[pT], constb], [QmTb[tt]])
                copy("dve", KT[:, :, c0:c0 + 128], pv[:, 512:1024].rearrange("p (h t) -> p h t", h=4),
                     [bankb[pT]], [KTt[tt]])

                pG = 7
                for ch in range(2):
                    for gi in range(2):
                        g = 2 * ch + gi
                        mm_group(bank[pG][gi * 64:(gi + 1) * 64, ch * 128:(ch + 1) * 128],
                                 [(vn[:, g * 64:(g + 1) * 64], wsT[:, g, :])],
                                 [vnb, wsTb], [bankb[pG]])
                tg, tgb = tFr.next()
                dve_tt(tg[:, 0:256], bank[pG][:, 0:256], PF[:, O_BSP:O_BSP + 256], ALU.add,
                       [bankb[pG], PFb], [tgb])
                dve_tt(bigT[:, 2:4, c0:c0 + 128], tg[:, 0:256].rearrange("p (c t) -> p c t", c=2),
                       uG[:, :, c0:c0 + 128], ALU.mult, [tgb, uGb[0][bi], uGb[1][bi]],
                       [bigb[2][bi], bigb[3][bi]])


            def run_interleaved(gens):
                gens = list(gens)
                while gens:
                    for g_ in list(gens):
                        if next(g_, "done") == "done":
                            gens.remove(g_)

            for s_ in range(NTT + 3):
                if s_ < NTT:
                    stageA(s_)
                if 0 <= s_ - 3 < NTT:
                    stageC(s_ - 3)
                gl = []
                if 0 <= s_ - 1 < NTT:
                    gl.append(genB1(s_ - 1))
                if 0 <= s_ - 2 < NTT:
                    gl.append(genB2(s_ - 2))
                run_interleaved(gl)

            chk('tok')
            def conv_gen():
                uflat = uG[:, :, :].rearrange("p a b -> p (a b)").bitcast(F32)
                sqT, meanT = uflat[:, 0:512], uflat[:, 512:1024]
                dgall = dg8[:, :]
                alluG = uGb[0] + uGb[1]
                S.add("dve", lambda e: e.memset(dummy[:, 0:1], 0.0), writes=alluG + [cvtb, cvt2b, dummyb])
                for j in range(2):
                    hb = hcpb[j]
                    dve_ts(hcpad[:, j, 1:4, 0:15], hcpad[:, j, 0:3, 256:271], flag[:, 0:1], None, ALU.mult, None,
                           hb + [constb], hb)
                    dve_ts(hcpad[:, j, 0:3, 271:286], hcpad[:, j, 1:4, 15:30], flag[:, 0:1], None, ALU.mult, None,
                           hb + [constb], hb)
                    yield
                di = 0
                lnstate = {"done": -100, "last_evac": -100}
                p7, p7b = bank[7], bankb[7]
                for ui, (bi, j) in enumerate([(b_, j_) for b_ in range(3) for j_ in range(2)]):
                    t0, n = BLK[bi]
                    segs = (2 * bi, 2 * bi + 2) if bi < 2 else (4, 5)
                    cb = 2 + (ui % 2)
                    AH = 6
                    for step in range(31 + AH):
                        if step < 31:
                            k = step
                            sl = (di + k) % 8
                            dg = dgall[:, sl * 128:(sl + 1) * 128]
                            wk = PF[:, O_WDW + j * 31 + k:O_WDW + j * 31 + k + 1]
                            S.add("dve", lambda e, dg=dg, wk=wk: e.tensor_scalar(
                                out=dg, in0=ident[:, :], scalar1=wk, scalar2=None, op0=ALU.mult),
                                reads=[constb, PFb], writes=[dgb[sl]])
                        if step >= AH:
                            k = step - AH
                            sl = (di + k) % 8
                            dg = dgall[:, sl * 128:(sl + 1) * 128]
                            S.add("pe", (lambda e, cb=cb, n=n, dg=dg, j=j, segs=segs, k=k: e.matmul(
                                bank[cb][:, 0:n], dg, hcpad[:, j, segs[0]:segs[1], k:k + 256],
                                start=(k == 0), stop=(k == 30))),
                                reads=[dgb[sl], hcpb[j][bi]], writes=[bankb[cb]])
                        yield
                    di += 31

                    evp[0] += 1

                    def evac(j=j, cb=cb, n=n):
                        evp[0] -= 1
                        S.add("act", lambda e: e.activation(
                            out=cacc[:, j, 0:n], in_=bank[cb][:, 0:n], func=AF.Identity,
                            bias=PF[:, O_BDW + j:O_BDW + j + 1]),
                            reads=[bankb[cb], PFb], writes=[caccb])
                    cdef.append((max(cur["i"] + 8, lnstate["done"] + 2), evac))
                    lnstate["last_evac"] = max(cur["i"] + 8, lnstate["done"] + 2)
                    if j == 0:
                        continue
                    sqA = sqT.bitcast(BF16)[:, 0:512]
                    sqB = sqT.bitcast(BF16)[:, 512:1024]

                    def s0(n=n):
                        mm_group(p7[:, 0:n], [(onesf[:, :], cacc[:, jj, 0:n]) for jj in range(2)],
                                 [caccb, constb], [p7b])
                        dve_ts(meanT[:, 0:n], p7[:, 0:n], 1.0 / 256, None, ALU.mult, None, [p7b], [cvtb])

                    def s1(n=n):
                        for jj, sq_ in ((0, sqA), (1, sqB)):
                            dve_tt(cacc[:, jj, 0:n], cacc[:, jj, 0:n], meanT[:, 0:n], ALU.subtract,
                                   [caccb, cvtb], [caccb], eng="pool")
                            dve_tt(sq_[:, 0:n], cacc[:, jj, 0:n], cacc[:, jj, 0:n], ALU.mult, [caccb], [cvt2b],
                                   eng="pool")

                    def s2(n=n):
                        mm_group(p7[:, 0:n], [(onesb[:, :], sqA[:, 0:n]), (onesb[:, :], sqB[:, 0:n])],
                                 [cvt2b, constb], [p7b])
                        act(meanT[:, 0:n], p7[:, 0:n], AF.Ln, [p7b, constb], [cvtb], scale=1.0 / 256,
                            bias=epsb[:, 0:1])

                    def s3(n=n):
                        act(meanT[:, 0:n], meanT[:, 0:n], AF.Exp, [cvtb], [cvtb], scale=-0.5)

                    def s4(n=n):
                        for jj in range(2):
                            dve_tt(cacc[:, jj, 0:n], cacc[:, jj, 0:n], meanT[:, 0:n], ALU.mult, [caccb, cvtb], [caccb],
                                   eng="pool")

                    def s5(n=n, t0=t0, bi=bi):
                        for jj in range(2):
                            S.add("act", lambda e, jj=jj: e.activation(
                                out=bigT[:, 4 + jj, t0:t0 + n], in_=cacc[:, jj, 0:n], func=AF.Silu,
                                scale=PF[:, O_GCV + jj:O_GCV + jj + 1], bias=PF[:, O_BCV + jj:O_BCV + jj + 1]),
                                reads=[caccb, PFb], writes=[bigb[4 + jj][bi]])
                    base_ = lnstate["last_evac"] + 4
                    for si, st_fn in enumerate((s0, s1, s2, s3, s4, s5)):
                        cdef.append((base_ + 4 * si + (4 if si >= 2 else 0), st_fn))
                    lnstate["done"] = base_ + 4 * 5 + 4
                S.add("dve", lambda e: e.memset(dummy[:, 0:1], 0.0), writes=[cvtb, cvt2b] + alluG + [dummyb])
                yield

            groupsA, groupsB = [], []
            for h in range(4):
                for qb in range(2):
                    groupsA.append((h, qb * 512, 512, [10, 11] + list(range(8)), [4 * qb + i for i in range(4)], qb))
            for h in range(4):
                groupsB.append((h, 1024, 256, [8, 9], [8, 9], 2))
            groups = []
            for i_ in range(8):
                groups.append(groupsA[i_])
                if i_ < 4:
                    groups.append(groupsB[i_])
            Sbank = Ring([4, 5, 6])
            Oset = Ring([(0, 1)])
            items = []
            for gi, (h, q0, nq, kts, qtiles, bi) in enumerate(groups):
                ob = Oset.next()
                for c in range(2):
                    for ki, kt in enumerate(kts):
                        items.append((gi, c, ki, kt, ob))

            def kcols(kt):
                return (NT + (kt - 10) * 128) if kt >= 10 else kt * 128

            pend = []

            def emit_S(it):
                gi, c, ki, kt, ob = it
                h, q0, nq, kts, qtiles, bi = groups[gi]
                sbk = Sbank.next()
                k0 = kcols(kt)
                if DUMMY_N:
                    S.add("pe", lambda e, sbk=sbk: e.matmul(bank[sbk][:, 0:DUMMY_N], ident[:, :], hT[:, 0, 0:DUMMY_N],
                                                            start=True, stop=True),
                          reads=[constb, hTb[0][0]], writes=[bankb[sbk]])
                mm_group(bank[sbk][:, 0:nq],
                         [(KT[:, h, k0:k0 + 128], hT[:, 4 * c + h, q0:q0 + nq])],
                         [KTt[kt], hTb[4 * c + h][bi]] + [QmTb[t] for t in qtiles], [bankb[sbk]])
                E, Eb = Err.next()
                act(E[:, 0:nq], bank[sbk][:, 0:nq], AF.Exp, [bankb[sbk]], [Eb], scale=QSCALE)
                pend.append((it, E, Eb))

            def emit_PV(it, E, Eb):
                gi, c, ki, kt, ob = it
                h, q0, nq, kts, qtiles, bi = groups[gi]
                last = (ki == len(kts) - 1)
                if ki == 0:
                    flush_O_readers(c)
                S.add("pe", lambda e: e.matmul(bank[ob[c]][:, 0:nq], Vaug[:, kt, h, :], E[:, 0:nq],
                                               start=(ki == 0), stop=last),
                      reads=[Vt[kt], Eb], writes=[bankb[ob[c]]])
                if last:
                    finalize(gi, ob, c)

            gq = []
            cur = {"i": 0}

            def live_T():
                return sum(g_["nT"] for g_ in gq)

            def run_group_until(g_, pred):
                while any(pred(st_) for st_ in g_["steps"]):
                    g_["steps"].pop(0)[2]()

            def flush_O_readers(c):
                for g_ in gq:
                    run_group_until(g_, lambda st_: st_[1] == c)

            def finalize(gi, ob, c):
                h, q0, nq, kts, qtiles, bi = groups[gi]
                nb = (h % 2) * 64
                db = 64 - nb
                nr = slice(nb, nb + 64)
                dr = slice(db, db + 64)
                Oc = bank[ob[c]]
                imm = False
                D = (lambda d: 0) if imm else (lambda d: d)
                t_ = cur["i"]
                nxt = tFr.i % len(tFr.items)
                for g0 in list(gq):
                    if nxt in g0["Tidx"]:
                        assert g0["done"], "T ring too small"
                        run_group_until(g0, lambda st_: True)
                        gq.remove(g0)
                T, Tb = tFr.next()
                if c == 0:
                    grp = {"gi": gi, "steps": [], "nT": 1, "done": False, "T0": (T, Tb), "Tidx": [nxt]}
                    gq.append(grp)
                else:
                    grp = [g_ for g_ in gq if g_["gi"] == gi][0]
                    grp["nT"] = 2
                    grp["done"] = True
                    grp["Tidx"].append(nxt)
                steps = grp["steps"]
                if c == 0 and not imm:
                    steps.append((t_ + D(1), c, lambda: S.add(
                        "dve", lambda e: e.reciprocal(out=T[nr, 0:nq], in_=Oc[dr, 0:nq]),
                        reads=[bankb[ob[c]]], writes=[Tb])))
                else:
                    steps.append((t_ + D(1), c,
                                  lambda: act(T[nr, 0:nq], Oc[dr, 0:nq], AF.Ln, [bankb[ob[c]]], [Tb])))
                    steps.append((t_ + D(3), None,
                                  lambda: act(T[nr, 0:nq], T[nr, 0:nq], AF.Exp, [Tb], [Tb], scale=-1.0)))
                steps.append((t_ + D(5), c,
                              lambda: dve_tt(T[nr, 0:nq], Oc[nr, 0:nq], T[nr, 0:nq], ALU.mult,
                                             [bankb[ob[c]], Tb], [Tb])))
                if c == 0:
                    return
                T0, T0b = grp["T0"]
                T1, T1b = T, Tb
                sq, sqb = tBr.next()

                def st_mix():
                    dve_stt(T0[nr, 0:nq], T1[nr, 0:nq], neglam[nr, :], T0[nr, 0:nq], ALU.mult, ALU.add,
                            [T0b, T1b, lamb], [T0b])
                    dve_tt(sq[nr, 0:nq], T0[nr, 0:nq], T0[nr, 0:nq], ALU.mult, [T0b], [sqb])

                def st_ss():
                    mm_group(bank[7][nr, 0:nq], [(onesb[nr, 0:64], sq[nr, 0:nq])], [sqb, constb], [bankb[7]])
                    act(T1[nr, 0:nq], bank[7][nr, 0:nq], AF.Ln, [bankb[7], constb], [T1b], scale=1.0 / 64,
                        bias=epsb[nr, 0:1])

                def st_exp():
                    act(T1[nr, 0:nq], T1[nr, 0:nq], AF.Exp, [T1b], [T1b], scale=-0.5)

                def st_out():
                    dve_stt(bigT[nr, h // 2, q0:q0 + nq], T0[nr, 0:nq], ghs[nr, :], T1[nr, 0:nq],
                            ALU.mult, ALU.mult, [T0b, T1b, lamb], [bigb[h // 2][bi]])
                steps.append((t_ + D(7), None, st_mix))
                steps.append((t_ + D(10), None, st_ss))
                steps.append((t_ + D(12), None, st_exp))
                steps.append((t_ + D(14), None, st_out))

            def run_deferred(force=False):
                for g_ in list(gq):
                    while g_["steps"] and (force or g_["steps"][0][0] <= cur["i"]):
                        g_["steps"].pop(0)[2]()
                    if g_["done"] and not g_["steps"]:
                        gq.remove(g_)
                    elif not force and g_["steps"]:
                        pass

            cdef = []
            evp = [0]

            def run_cdef(force=False):
                k_ = 0
                while k_ < len(cdef):
                    if force or cdef[k_][0] <= cur["i"]:
                        cdef.pop(k_)[1]()
                    else:
                        k_ += 1

            LOOK = 2
            def y1_gen():
                S.add("dve", lambda e: e.memset(dummy[:, 1:2], 0.0), writes=tmpall + QTt + [dummyb])
                for tt in range(NTT):
                    bi_ = min(tt // 4, 2)
                    p = 2 + (tt % 2)
                    for ch in range(2):
                        mm_group(bank[p][:, ch * 256:(ch + 1) * 256],
                                 [(zdT[:, ch, tt * 128:(tt + 1) * 128], csc[:, :])],
                                 [zdTb[ch][bi_], constb], [bankb[p]])
                    copy("dve", Y1[:, tt, :, :],
                         bank[p][:, :].rearrange("p (c n) -> p c n", c=2), [bankb[p]], QTt)
                    yield

            def mg_chain():
                if l == 0:
                    yield from mod_gen(0, 4, 12)
                if l + 1 < L:
                    yield from mod_gen(l + 1)
            mg = mg_chain()
            cg = conv_gen()

            def adv(g, n):
                for _ in range(n):
                    if next(g, "done") == "done":
                        return True
                return False

            yg = y1_gen()
            gstate = {"cg_done": False, "mg_done": False, "dsl": None}

            def issue_dsl():
                if gstate["dsl"] is None:
                    gstate["dsl"] = [load_slot(dfa_d[i_, :, :].rearrange("p (k n) -> p k n", k=8), cast=False)
                                     for i_ in range(4)]

            for i, it in enumerate(items):
                cur["i"] = i
                emit_S(it)
                if i >= LOOK:
                    emit_PV(*pend.pop(0))
                run_deferred()
                run_cdef()
                if not gstate["cg_done"]:
                    gstate["cg_done"] = adv(cg, 2 if i % 2 == 0 else 1)
                elif evp[0] == 0:
                    adv(yg, 1)
                if i % (6 if l == 0 else 8) == 4 and not gstate["mg_done"]:
                    gstate["mg_done"] = adv(mg, 1)
                    if gstate["mg_done"]:
                        issue_dsl()
            while pend:
                emit_PV(*pend.pop(0))
            run_deferred(force=True)
            adv(mg, 10000)
            for _ in range(400):
                cur["i"] += 1
                run_cdef()
                adv(cg, 2)
            run_cdef(force=True)
            adv(cg, 10000)
            run_cdef(force=True)
            issue_dsl()
            dsl_all = gstate["dsl"]
            adv(yg, 10000)

            chk('att')
            chk('conv')
            for th in range(2):
                dslots = dsl_all[2 * th:2 * th + 2]
                if th == 1:
                    wo = [load_slot(wslot_cols(w_out[l], c * 512)) for c in range(2)]
                cv_ = dslots[0][0][:, :].rearrange("p (k n) -> p k n", k=8)
                sv_ = dslots[1][0][:, :].rearrange("p (k n) -> p k n", k=8)
                for ch in range(2):
                    p = pbr.next()
                    pairs = [(Y1[:, tc, ch, 0:128], cv_[:, tc, :]) for tc in range(8)] + \
                            [(Y1[:, tc, ch, 128:256], sv_[:, tc, :]) for tc in range(8)]
                    mm_group(bank[p][:, :], pairs, QTt + [dslots[0][1], dslots[1][1]], [bankb[p]])
                    copy("act" if ch else "dve", bigT[:, 6 + ch, th * 512:(th + 1) * 512], bank[p][:, :],
                         [bankb[p]], [bigb[6 + ch][th]])
            for ch in range(2):
                p = pbr.next()
                pairs = [(Y1[:, 8 + tc, ch, 0:128], dfb[:, 0, tc, :]) for tc in range(2)] + \
                        [(Y1[:, 8 + tc, ch, 128:256], dfb[:, 1, tc, :]) for tc in range(2)]
                mm_group(bank[p][:, 0:256], pairs, QTt + [constb], [bankb[p]])
                copy("act" if ch else "dve", bigT[:, 6 + ch, 1024:1280], bank[p][:, 0:256],
                     [bankb[p]], [bigb[6 + ch][2]])

            if debug and l == 0:
                dbg_dump("mixcat", bigT[:, :, :], [128, 8, NT], [b for r in bigb for b in r], BF16)

            chk('fourier')
            pbr2 = Ring(list(range(3, 8)))
            w1n = [load_slot(wslot_cols(w_ff1[l], c * 512)) for c in range(2)]
            for bi, (t0, n) in enumerate(BLK):
                c = 0 if bi < 2 else 1
                for dt in range(8):
                    t, b = wo[dt // 4]
                    tv = t[:, :].rearrange("p (k n) -> p k n", k=8)
                    p = pbr2.next()
                    mm_group(bank[p][:, 0:n],
                             [(tv[:, k, (dt % 4) * 128:(dt % 4 + 1) * 128], bigT[:, k, t0:t0 + n]) for k in range(8)],
                             [bigb[k][bi] for k in range(8)] + [b], [bankb[p]])
                    dve_stt(xT[:, dt, t0:t0 + n], bank[p][:, 0:n], mod_ap(2, dt, c), xT[:, dt, t0:t0 + n],
                            ALU.mult, ALU.add, [bankb[p], modSb, xTb[dt][bi]], [xTb[dt][bi]])
                if bi >= 1:
                    norm_blk(2, bi - 1, l)
            w2n = [load_slot(wslot_rows(w_ff2[l], c * 512)) for c in range(2)]
            norm_blk(2, 2, l)
            if debug and l == 0:
                dbg_dump("x1", xT[:, :, :], [128, 8, NT], [b for r in xTb for b in r])

            chk('wout')
            for fq in range(4):
                w1 = w1n
                w2 = w2n if fq == 0 else [load_slot(wslot_rows(w_ff2[l], fq * 1024 + c * 512)) for c in range(2)]
                for fc in range(8):
                    t, b = w1[fc // 4]
                    tv = t[:, :].rearrange("p (k n) -> p k n", k=8)
                    for bi, (t0, n) in enumerate(BLK):
                        p = pbr2.next()
                        mm_group(bank[p][:, 0:n],
                                 [(tv[:, k, (fc % 4) * 128:(fc % 4 + 1) * 128], hT[:, k, t0:t0 + n]) for k in range(8)],
                                 hall(bi) + [b], [bankb[p]])
                        rb, rbb = tBr.next()
                        act(rb[:, 0:n], bank[p][:, 0:n], AF.Relu, [bankb[p]], [rbb])
                        dve_tt(bigT[:, fc, t0:t0 + n], rb[:, 0:n], rb[:, 0:n], ALU.mult, [rbb], [bigb[fc][bi]],
                               eng="pool")
                if fq < 3:
                    w1n = [load_slot(wslot_cols(w_ff1[l], (fq + 1) * 1024 + c * 512)) for c in range(2)]
                order = [(dt, bi) for dt in range(8) for bi in range(3)] if fq < 3 else \
                        [(dt, bi) for bi in range(3) for dt in range(8)]
                for (dt, bi) in order:
                    if True:
                        t0, n = BLK[bi]
                        c = 0 if bi < 2 else 1
                        p = pbr2.next()
                        pairs = []
                        for fc in range(8):
                            t, b = w2[fc // 4]
                            tv = t[:, :].rearrange("p (f n) -> p f n", f=4)
                            pairs.append((tv[:, fc % 4, dt * 128:(dt + 1) * 128], bigT[:, fc, t0:t0 + n]))
                        mm_group(bank[p][:, 0:n], pairs,
                                 [bigb[fc][bi] for fc in range(8)] + [w2[0][1], w2[1][1]], [bankb[p]])
                        dve_stt(xT[:, dt, t0:t0 + n], bank[p][:, 0:n], mod_ap(5, dt, c), xT[:, dt, t0:t0 + n],
                                ALU.mult, ALU.add, [bankb[p], modSb, xTb[dt][bi]], [xTb[dt][bi]])
                        if fq == 3 and dt == 7 and l + 1 < L and bi >= 1:
                            norm_blk(1, bi - 1, l + 1)
            if l + 1 < L:
                norm_blk(1, 2, l + 1)

        try:
            for l in range(L):
                layer(l)
        except _Stop:
            pass

        for k in range(8):
            S.add("sp", lambda e, k=k: e.dma_start(out=yT_d[k * 128:(k + 1) * 128, :], in_=xT[:, k, :]),
                  reads=xTb[k], dgrp=g_y)

        S.emit()
    return nc, dbg_d


def _bf(a):
    return np.ascontiguousarray(a).astype(ml_dtypes.bfloat16)


def _core_units(i):
    if i < 2:
        return True, [i], 24 + i
    return False, list(range(4 * (i - 2), 4 * (i - 2) + 4)), 24 + i


def _const_tables(is_sample):
    cs = np.zeros((NT, 32), np.float32)
    cs[:, 0:16] = 1.0
    if is_sample:
        pos = np.arange(1024)
        row = (pos // 64).astype(np.float32)
        col = (pos % 64).astype(np.float32)
        nf = 8
        inv = (10000.0 ** (-np.arange(nf, dtype=np.float32) / nf)).astype(np.float32)
        ang = np.stack([row[:, None] * inv, col[:, None] * inv], axis=1).astype(np.float32)
        cs[0:1024, 0:16] = np.cos(ang).reshape(1024, 16)
        cs[0:1024, 16:32] = np.sin(ang).reshape(1024, 16)
    cs = cs.reshape(NTT, 128, 32).transpose(1, 0, 2)
    ind = np.zeros((12, 128, 2, 8), np.float32)
    if not is_sample:
        for tt in range(8):
            seg = tt // 2
            ind[tt, :, 0, seg] = 1.0
            ind[tt, :, 1, 0:4] = -BIG
            ind[tt, :, 1, seg] = 0.0
        ind[10:12, :, 1, 0:4] = -BIG
    ind = ind.transpose(1, 0, 2, 3)
    flag = np.full((128, 1), 1.0 if is_sample else 0.0, np.float32)
    c = np.arange(64)
    ang = 2 * np.pi * np.outer(c, c) / 64.0
    csc = np.zeros((128, 256), np.float64)
    for g in range(2):
        csc[g * 64:(g + 1) * 64, g * 64:(g + 1) * 64] = np.cos(ang) / 8.0
        csc[g * 64:(g + 1) * 64, 128 + g * 64:128 + (g + 1) * 64] = np.sin(ang) / 8.0
    Lq = 1024 if is_sample else 256
    t = np.arange(Lq)
    a = 2 * np.pi * np.outer(t, t) / Lq
    Cq, Sq = np.cos(a) / math.sqrt(Lq), -np.sin(a) / math.sqrt(Lq)
    CA = np.zeros((1024, 1024))
    SA = np.zeros((1024, 1024))
    for s in range(1024 // Lq):
        CA[s * Lq:(s + 1) * Lq, s * Lq:(s + 1) * Lq] = Cq
        SA[s * Lq:(s + 1) * Lq, s * Lq:(s + 1) * Lq] = Sq
    dfa = np.zeros((4, 128, 8, 512))
    for th in range(2):
        dfa[2 * th + 0] = CA[:, th * 512:(th + 1) * 512].reshape(8, 128, 512).transpose(1, 0, 2)
        dfa[2 * th + 1] = SA[:, th * 512:(th + 1) * 512].reshape(8, 128, 512).transpose(1, 0, 2)
    dfa = dfa.reshape(4, 128, 4096)
    t = np.arange(256)
    a = 2 * np.pi * np.outer(t, t) / 256.0
    Cb, Sb = np.cos(a) / 16.0, -np.sin(a) / 16.0
    dfb = np.zeros((128, 2, 2, 256))
    dfb[:, 0] = Cb.reshape(2, 128, 256).transpose(1, 0, 2)
    dfb[:, 1] = Sb.reshape(2, 128, 256).transpose(1, 0, 2)
    return dict(cs=np.ascontiguousarray(cs), indt=_bf(ind), flag=flag, csc=_bf(csc), dfa=_bf(dfa),
                dfb=_bf(dfb), ident=_bf(np.eye(128)))


def _param_image(inp):
    pf = np.zeros((DEPTH, 128, NF), np.float32)
    p = np.arange(128)
    for l in range(DEPTH):
        pf[l, :, O_BADA:O_BADA + 48] = inp["b_ada"][l].reshape(48, 128).T
        pf[l, :, O_GATT:O_GATT + 8] = inp["g_attn_norm"][l].reshape(8, 128).T
        pf[l, :, O_GMLP:O_GMLP + 8] = inp["g_mlp_norm"][l].reshape(8, 128).T
        pf[l, :, O_GQK:O_GQK + 32] = inp["g_q"][l][None, :]
        pf[l, :, O_GQK + 32:O_GQK + 64] = inp["g_k"][l][None, :]
        pf[l, :, O_GSG:O_GSG + 256] = inp["g_sg"][l][None, :]
        pf[l, :, O_BSG:O_BSG + 256] = inp["b_sg"][l][None, :]
        pf[l, :, O_GHEAD] = inp["g_head"][l][p % 64]
        pf[l, :, O_LAM:O_LAM + 32] = inp["lam_q1"][l][None, :]
        pf[l, :, O_LAM + 32:O_LAM + 64] = inp["lam_k1"][l][None, :]
        pf[l, :, O_LAM + 64:O_LAM + 96] = inp["lam_q2"][l][None, :]
        pf[l, :, O_LAM + 96:O_LAM + 128] = inp["lam_k2"][l][None, :]
        for j in range(2):
            pf[l, :, O_WDW + j * 31:O_WDW + (j + 1) * 31] = inp["w_dw"][l][:, j * 128:(j + 1) * 128].T
            pf[l, :, O_BDW + j] = inp["b_dw"][l][j * 128:(j + 1) * 128]
            pf[l, :, O_GCV + j] = inp["g_conv"][l][j * 128:(j + 1) * 128]
            pf[l, :, O_BCV + j] = inp["b_conv"][l][j * 128:(j + 1) * 128]
            for gi in range(2):
                pf[l, gi * 64:(gi + 1) * 64, O_BSP + j * 128:O_BSP + (j + 1) * 128] = \
                    inp["b_spatial"][l][2 * j + gi][None, :]
    wsT = np.ascontiguousarray(np.transpose(inp["w_spatial"], (0, 3, 1, 2))).reshape(DEPTH, 128, 512)
    return pf, wsT


def make_in_maps(inp):
    inp = {k: np.asarray(v) for k, v in inp.items()}
    pf, wsT = _param_image(inp)
    pa = np.ascontiguousarray(pf[:, :, 0:64])
    shared = dict(pf=pf, pa=pa, wsT=wsT.astype(np.float32),
                  w_ada=np.ascontiguousarray(inp["w_ada"], dtype=np.float32),
                  w_in=np.ascontiguousarray(inp["w_in"], dtype=np.float32),
                  w_out=np.ascontiguousarray(inp["w_out"], dtype=np.float32),
                  w_ff1=np.ascontiguousarray(inp["w_ff1"], dtype=np.float32),
                  w_ff2=np.ascontiguousarray(inp["w_ff2"], dtype=np.float32))
    tabs = {True: _const_tables(True), False: _const_tables(False)}
    maps = []
    for i in range(8):
        is_s, seqs, bseq = _core_units(i)
        if is_s:
            xa = inp["x_sample"][seqs[0]]
            condA = inp["c"][seqs[0]]
            ck = inp["cache_k"][seqs[0]].reshape(DEPTH, 256, 256)
            cv = inp["cache_v"][seqs[0]].reshape(DEPTH, 256, 256)
        else:
            xa = inp["x_prompt"][seqs].reshape(1024, D)
            condA = inp["c_ctx"]
            ck = np.zeros((DEPTH, 256, 256), np.float32)
            cv = np.zeros((DEPTH, 256, 256), np.float32)
        x = np.concatenate([xa, inp["x_prompt"][bseq]], axis=0)
        cond = np.stack([condA, inp["c_ctx"]], axis=1)
        m = dict(shared)
        m.update(tabs[is_s])
        m["xT"] = np.ascontiguousarray(x.T, dtype=np.float32)
        m["condL"] = np.ascontiguousarray(cond.reshape(8, 128, 2).transpose(1, 0, 2), dtype=np.float32)
        m["ck"] = np.ascontiguousarray(ck, dtype=np.float32)
        m["cv"] = np.ascontiguousarray(cv, dtype=np.float32)
        maps.append(m)
    return maps


_NC_CACHE = {}


def kernel(**inputs):
    if "nc" not in _NC_CACHE:
        _NC_CACHE["nc"] = build()[0]
    nc = _NC_CACHE["nc"]
    maps = make_in_maps(inputs)
    res = run_bass_kernel_spmd(nc, maps, core_ids=list(range(8)))
    B, SEQ = 32, 256
    y_p = np.zeros((B, SEQ, D), np.float32)
    y_s = np.zeros((2, 1024, D), np.float32)
    nk = np.zeros((B, DEPTH, SEQ, 4, 64), np.float32)
    nv = np.zeros((B, DEPTH, SEQ, 4, 64), np.float32)
    for i in range(8):
        r = res.results[i]
        y = np.asarray(r["yT"]).T
        k_ = np.asarray(r["nk"])
        v_ = np.asarray(r["nv"])
        is_s, seqs, bseq = _core_units(i)
        if is_s:
            y_s[seqs[0]] = y[0:1024]
        else:
            for j, s in enumerate(seqs):
                y_p[s] = y[j * 256:(j + 1) * 256]
                nk[s] = k_[:, j * 256:(j + 1) * 256].reshape(DEPTH, SEQ, 4, 64)
                nv[s] = v_[:, j * 256:(j + 1) * 256].reshape(DEPTH, SEQ, 4, 64)
        y_p[bseq] = y[1024:1280]
        nk[bseq] = k_[:, 1024:1280].reshape(DEPTH, SEQ, 4, 64)
        nv[bseq] = v_[:, 1024:1280].reshape(DEPTH, SEQ, 4, 64)
    return (y_p, y_s, nk, nv)
```
